# Optimizing a Trainium2 kernel written in Bass

```python
import math
import jax, jax.numpy as jnp
from jax import lax
import numpy as np

D_MODEL = 1024
BATCH = 32
SEQ = 2048
DEPTH = 1

GRID_W = 64
CTX_LEN = 256
MLA_HEADS = 8
QK_NOPE = 64
QK_ROPE = 32
QK_DIM = QK_NOPE + QK_ROPE
V_DIM = 64
Q_LORA = 384
KV_LORA = 256
ROPE_THETA = 10000.0
Q_BLOCK = 128
F_GROUPS = 4
F_GROUP_DIM = 128
D_F = F_GROUPS * F_GROUP_DIM
OFF_Q = 0
OFF_KV = OFF_Q + Q_LORA
OFF_KPE = OFF_KV + KV_LORA
OFF_F = OFF_KPE + QK_ROPE
OFF_GA = OFF_F + D_F
OFF_GB = OFF_GA + D_MODEL
N_IN = OFF_GB + D_MODEL
N_EXPERTS = 16
D_EXPERT = 512
CAPACITY_FACTOR = 2
EPS = 1e-6

kernel_name = "hybrid_mla_fnet_ecmoe_dit_layer"


def _rms(x, g):
    xf = x.astype(jnp.float32)
    y = xf * lax.rsqrt(jnp.mean(xf * xf, axis=-1, keepdims=True) + EPS)
    return (y * g.astype(jnp.float32)).astype(x.dtype)


def _modulate(h, shift, scale):
    return h * (1.0 + scale) + shift


def _axial_tables(pos_row, pos_col, dtype):
    half = QK_ROPE // 2
    n_freq = half // 2
    inv_freq = 1.0 / (ROPE_THETA ** (jnp.arange(n_freq, dtype=jnp.float32) / n_freq))

    def tab(pos):
        ang = pos.astype(jnp.float32)[:, None] * inv_freq[None, :]
        ang = jnp.concatenate([ang, ang], axis=-1)[None, :, None, :]
        return jnp.cos(ang).astype(dtype), jnp.sin(ang).astype(dtype)

    cr, sr = tab(pos_row)
    cc, scol = tab(pos_col)
    return cr, sr, cc, scol


def _rope_axis(x, cos, sin):
    x1, x2 = jnp.split(x, 2, axis=-1)
    return x * cos + jnp.concatenate([-x2, x1], axis=-1) * sin


def _rope_2d(x, tabs):
    cr, sr, cc, scol = tabs
    xr, xc = jnp.split(x, 2, axis=-1)
    return jnp.concatenate([_rope_axis(xr, cr, sr), _rope_axis(xc, cc, scol)], axis=-1)


def _mla_qkv(p, q_a_g, kv_a_g, w_q_up, w_kv_up, q_g, k_g, tabs):
    B, T, _ = p.shape
    cq = _rms(p[..., OFF_Q:OFF_KV], q_a_g)
    ckv = _rms(p[..., OFF_KV:OFF_KPE], kv_a_g)
    k_pe = p[..., OFF_KPE:OFF_F]
    q = (cq @ w_q_up).reshape(B, T, MLA_HEADS, QK_DIM)
    kv = (ckv @ w_kv_up).reshape(B, T, MLA_HEADS, QK_NOPE + V_DIM)
    k_nope, v = kv[..., :QK_NOPE], kv[..., QK_NOPE:]
    k = jnp.concatenate([k_nope, jnp.broadcast_to(k_pe[:, :, None, :], (B, T, MLA_HEADS, QK_ROPE))], axis=-1)
    q = _rms(q, q_g)
    k = _rms(k, k_g)
    if tabs is not None:
        q = jnp.concatenate([q[..., :QK_NOPE], _rope_2d(q[..., QK_NOPE:], tabs)], axis=-1)
        k = jnp.concatenate([k[..., :QK_NOPE], _rope_2d(k[..., QK_NOPE:], tabs)], axis=-1)
    return q, k, v


def _attend_dense(q, k, v):
    s = jnp.einsum('bqhd,bkhd->bhqk', q, k, preferred_element_type=jnp.float32) * (QK_DIM ** -0.5)
    pr = jax.nn.softmax(s, axis=-1).astype(v.dtype)
    return jnp.einsum('bhqk,bkhd->bqhd', pr, v)


def _attend_latent(q, k_lat, v_lat, k_ctx, v_ctx):
    B, T, H, _ = q.shape
    k_all = jnp.concatenate([k_lat, k_ctx], axis=1)
    v_all = jnp.concatenate([v_lat, v_ctx], axis=1)
    nb = T // Q_BLOCK
    qb = jnp.moveaxis(q.reshape(B, nb, Q_BLOCK, H, QK_DIM), 1, 0)
    o = lax.map(lambda qi: _attend_dense(qi, k_all, v_all), qb)
    return jnp.moveaxis(o, 0, 1).reshape(B, T, H * V_DIM)


def _fourier(f):
    B, T, _ = f.shape
    fg = f.reshape(B, T, F_GROUPS, F_GROUP_DIM).astype(jnp.float32)
    y = jnp.fft.fft2(fg, axes=(1, 3), norm="ortho").real
    return y.reshape(B, T, D_F).astype(f.dtype)


def _expert_choice(h, w_r, w_g, w_u, w_d):
    B, T, D = h.shape
    cap = (CAPACITY_FACTOR * T) // N_EXPERTS
    logits = jnp.einsum('btd,de->bte', h, w_r, preferred_element_type=jnp.float32)
    aff = jax.nn.softmax(logits, axis=-1)
    gate, idx = lax.top_k(jnp.swapaxes(aff, 1, 2), cap)
    xe = jax.vmap(lambda hb, ib: hb[ib])(h, idx)
    a = jnp.einsum('becd,edf->becf', xe, w_g)
    u = jnp.einsum('becd,edf->becf', xe, w_u)
    ye = jnp.einsum('becf,efd->becd', jax.nn.silu(a) * u, w_d) * gate[..., None].astype(h.dtype)
    return jax.vmap(lambda yb, ib: jnp.zeros((T, D), yb.dtype).at[ib.reshape(-1)].add(yb.reshape(-1, D)))(ye, idx)


def _merge(p, attn_o, four_o, w_o_attn, w_fourier, w_out):
    ga = jax.nn.sigmoid(p[..., OFF_GA:OFF_GB])
    gb = jax.nn.sigmoid(p[..., OFF_GB:N_IN])
    return (ga * (attn_o @ w_o_attn) + gb * (four_o @ w_fourier)) @ w_out


def setup_inputs(seed: int = 0) -> dict:
    key = jax.random.key(seed)
    ks = jax.random.split(key, 24)
    L, D = DEPTH, D_MODEL

    def nrm(k, shape, fan_in, mult=1.0):
        return jax.random.normal(k, shape, jnp.float32) * (mult * fan_in ** -0.5)

    def gain(k, shape):
        return 1.0 + 0.02 * jax.random.normal(k, shape, jnp.float32)

    return {
        "x": jax.random.normal(ks[0], (BATCH, SEQ, D), jnp.float32),
        "c": jax.random.normal(ks[1], (BATCH, D), jnp.float32),
        "ctx": jax.random.normal(ks[2], (BATCH, CTX_LEN, D), jnp.float32),
        "c_ctx": 0.5 * jax.random.normal(ks[3], (D,), jnp.float32),
        "w_mod": nrm(ks[4], (L, D, 6 * D), D, 0.5),
        "b_mod": 0.02 * jax.random.normal(ks[5], (L, 6 * D), jnp.float32),
        "norm1_g": gain(ks[6], (L, D)),
        "w_in": nrm(ks[7], (L, D, N_IN), D),
        "q_a_norm_g": gain(ks[8], (L, Q_LORA)),
        "kv_a_norm_g": gain(ks[9], (L, KV_LORA)),
        "w_q_up": nrm(ks[10], (L, Q_LORA, MLA_HEADS * QK_DIM), Q_LORA),
        "w_kv_up": nrm(ks[11], (L, KV_LORA, MLA_HEADS * (QK_NOPE + V_DIM)), KV_LORA),
        "q_norm_g": gain(ks[12], (L, QK_DIM)),
        "k_norm_g": gain(ks[13], (L, QK_DIM)),
        "w_o_attn": nrm(ks[14], (L, MLA_HEADS * V_DIM, D), MLA_HEADS * V_DIM),
        "w_fourier": nrm(ks[15], (L, D_F, D), D_F),
        "w_out": nrm(ks[16], (L, D, D), D),
        "norm2_g": gain(ks[17], (L, D)),
        "w_router": nrm(ks[18], (L, D, N_EXPERTS), D),
        "w_e_gate": nrm(ks[19], (L, N_EXPERTS, D, D_EXPERT), D),
        "w_e_up": nrm(ks[20], (L, N_EXPERTS, D, D_EXPERT), D),
        "w_e_down": nrm(ks[21], (L, N_EXPERTS, D_EXPERT, D), D_EXPERT),
    }


def reference(x, c, ctx, c_ctx, w_mod, b_mod, norm1_g, w_in, q_a_norm_g, kv_a_norm_g,
              w_q_up, w_kv_up, q_norm_g, k_norm_g, w_o_attn, w_fourier, w_out, norm2_g,
              w_router, w_e_gate, w_e_up, w_e_down):
    T = x.shape[1]
    rows = T // GRID_W
    pos_row = jnp.repeat(jnp.arange(rows, dtype=jnp.int32), GRID_W)
    pos_col = jnp.tile(jnp.arange(GRID_W, dtype=jnp.int32), rows)
    tabs = _axial_tables(pos_row, pos_col, x.dtype)
    s_lat = jax.nn.silu(c)
    s_ctx = jax.nn.silu(c_ctx)

    for l in range(DEPTH):
        last = l == DEPTH - 1
        mod_l = (s_lat @ w_mod[l] + b_mod[l])[:, None, :]
        mod_c = s_ctx @ w_mod[l] + b_mod[l]
        sh1, sc1, g1, sh2, sc2, g2 = jnp.split(mod_l, 6, axis=-1)
        csh1, csc1, cg1, csh2, csc2, cg2 = jnp.split(mod_c, 6, axis=-1)

        h = _modulate(_rms(x, norm1_g[l]), sh1, sc1)
        hc = _modulate(_rms(ctx, norm1_g[l]), csh1, csc1)
        p = h @ w_in[l]
        pc = hc @ w_in[l]
        mla_w = (q_a_norm_g[l], kv_a_norm_g[l], w_q_up[l], w_kv_up[l], q_norm_g[l], k_norm_g[l])
        q, k, v = _mla_qkv(p, *mla_w, tabs)
        qc, kc, vc = _mla_qkv(pc, *mla_w, None)
        attn_lat = _attend_latent(q, k, v, kc, vc)
        four_lat = _fourier(p[..., OFF_F:OFF_GA])
        x = x + g1 * _merge(p, attn_lat, four_lat, w_o_attn[l], w_fourier[l], w_out[l])

        if not last:
            B = ctx.shape[0]
            attn_ctx = _attend_dense(qc, kc, vc).reshape(B, CTX_LEN, MLA_HEADS * V_DIM)
            four_ctx = _fourier(pc[..., OFF_F:OFF_GA])
            ctx = ctx + cg1 * _merge(pc, attn_ctx, four_ctx, w_o_attn[l], w_fourier[l], w_out[l])

        moe_w = (w_router[l], w_e_gate[l], w_e_up[l], w_e_down[l])
        h2 = _modulate(_rms(x, norm2_g[l]), sh2, sc2)
        x = x + g2 * _expert_choice(h2, *moe_w)
        if not last:
            hc2 = _modulate(_rms(ctx, norm2_g[l]), csh2, csc2)
            ctx = ctx + cg2 * _expert_choice(hc2, *moe_w)
    return x
```

```python
import numpy as np
import ml_dtypes
from contextlib import ExitStack
import concourse.bass as bass
import concourse.mybir as mybir
from concourse.bass_utils import run_bass_kernel_spmd

F32 = mybir.dt.float32
BF16 = mybir.dt.bfloat16
I32 = mybir.dt.int32
ALU = mybir.AluOpType
AF = mybir.ActivationFunctionType
AX = mybir.AxisListType

NB = 4
T = 2048
D = 1024
CTX = 256
NT = T // 128
H = 8
QKD = 96
NIN = 3232
OFF_Q, OFF_KV, OFF_KPE, OFF_F, OFF_GA, OFF_GB = 0, 384, 640, 672, 1184, 2208
NE = 16
CAP = 256
EPS = 1e-6

SEM_LIMIT = 30000
SAME_ENGINE_SYNC = True


class Buf:
    __slots__ = ("w", "r", "name")

    def __init__(self, name=""):
        self.w = {}
        self.r = {}
        self.name = name


def bufs(n):
    return [Buf() for _ in range(n)]


class _Eng:
    def __init__(self, name, ndma):
        self.name = name
        self.sems = []
        self.count = 0
        self.waited = {}
        self.prog = []
        self.dma_sems = []
        self.dma_vals = []
        self.dma_next = 0
        self.ndma = ndma


class Sched:
    def __init__(self, nc, stack):
        self.nc = nc
        self.stack = stack
        ndma = {"sp": 32, "act": 8, "pool": 32, "pe": 0, "dve": 0}
        self.eng = {n: _Eng(n, ndma[n]) for n in ("pe", "act", "dve", "pool", "sp")}
        for e in self.eng.values():
            self._new_sem(e)
            for i in range(e.ndma):
                s = stack.enter_context(nc.semaphore(f"d_{e.name}{i}"))
                e.dma_sems.append(s)
                e.dma_vals.append(0)
        self.ninst = 0

    def _new_sem(self, e):
        s = self.stack.enter_context(self.nc.semaphore(f"e_{e.name}{len(e.sems)}"))
        e.sems.append(s)
        e.count = 0

    def _deps(self, e, reads, writes, partial):
        deps = {}

        def upd(d):
            for s, v in d.items():
                if deps.get(s, 0) < v:
                    deps[s] = v

        for b in reads:
            upd(b.w)
        for b in writes:
            upd(b.w)
            upd(b.r)
        for b in partial:
            upd(b.r)
        waits = []
        own = e.sems
        for s, v in deps.items():
            if (e.name == "pe" or not SAME_ENGINE_SYNC) and any(s is o for o in own):
                continue
            if e.waited.get(s, 0) < v:
                e.waited[s] = v
                waits.append((s, v))
        return waits

    def _mark(self, sem, val, reads, writes, partial):
        for b in reads:
            if b.r.get(sem, 0) < val:
                b.r[sem] = val
        for b in writes:
            b.w = {sem: val}
            b.r = {}
        for b in partial:
            if b.w.get(sem, 0) < val:
                b.w[sem] = val

    def op(self, en, fn, reads=(), writes=(), partial=()):
        e = self.eng[en]
        waits = self._deps(e, reads, writes, partial)
        if e.count >= SEM_LIMIT:
            self._new_sem(e)
        e.count += 1
        sem, val = e.sems[-1], e.count

        def emit(eo, waits=waits, fn=fn, sem=sem):
            for s, v in waits:
                eo.wait_ge(s, v)
            fn(eo).then_inc(sem, 1)

        e.prog.append(emit)
        self.ninst += 1
        self._mark(sem, val, reads, writes, partial)

    def dma(self, en, fn, reads=(), writes=(), partial=()):
        e = self.eng[en]
        waits = self._deps(e, reads, writes, partial)
        k = e.dma_next
        e.dma_next = (k + 1) % e.ndma
        sem = e.dma_sems[k]
        prev = e.dma_vals[k]
        if prev > 0 and e.waited.get(sem, 0) < prev:
            e.waited[sem] = prev
            waits.append((sem, prev))
        val = prev + 16
        e.dma_vals[k] = val

        def emit(eo, waits=waits, fn=fn, sem=sem):
            for s, v in waits:
                eo.wait_ge(s, v)
            fn(eo).then_inc(sem, 16)

        e.prog.append(emit)
        self.ninst += 1
        self._mark(sem, val, reads, writes, partial)

    def _all_ticks(self):
        ticks = []
        for e in self.eng.values():
            for i, s in enumerate(e.sems):
                v = e.count if i == len(e.sems) - 1 else SEM_LIMIT
                if v > 0:
                    ticks.append((s, v))
            for s, v in zip(e.dma_sems, e.dma_vals):
                if v > 0:
                    ticks.append((s, v))
        return ticks

    def barrier(self):
        ticks = self._all_ticks()
        for e in self.eng.values():
            ws = []
            for s, v in ticks:
                if e.waited.get(s, 0) < v:
                    e.waited[s] = v
                    ws.append((s, v))

            def emit(eo, ws=ws):
                for s, v in ws:
                    eo.wait_ge(s, v)

            e.prog.append(emit)

    def finish(self):
        self.barrier()
        nc = self.nc
        with nc.Block() as block:
            @block.sync
            def _(eo):
                for f in self.eng["sp"].prog:
                    f(eo)

            @block.tensor
            def _(eo):
                for f in self.eng["pe"].prog:
                    f(eo)

            @block.scalar
            def _(eo):
                for f in self.eng["act"].prog:
                    f(eo)

            @block.vector
            def _(eo):
                for f in self.eng["dve"].prog:
                    f(eo)

            @block.gpsimd
            def _(eo):
                for f in self.eng["pool"].prog:
                    f(eo)


def build_program(stop_after="G", dbg=(), skip=()):
    nc = bass.Bass("TRN2", target_bir_lowering=False)

    def din(name, shape, dt):
        return nc.dram_tensor(name, list(shape), dt, kind="ExternalInput").ap()

    def dscr(name, shape, dt):
        kind = "ExternalOutput" if name in dbg else "Internal"
        return nc.dram_tensor(name, list(shape), dt, kind=kind).ap()

    x4 = din("x4", [NB, T, D], F32)
    ctx4 = din("ctx4", [NB, CTX, D], F32)
    cT = din("cT", [128, 40], F32)
    bmod5 = din("bmod5", [5, 6 * D], F32)
    n1g = din("n1g", [128, 8], F32)
    n2g = din("n2g", [128, 8], F32)
    n2g_bc = din("n2g_bc", [128, D], F32)
    qag = din("qag", [128, 3], F32)
    kvag = din("kvag", [128, 2], F32)
    qg_bc = din("qg_bc", [128, QKD], F32)
    kg_bc = din("kg_bc", [128, QKD], F32)
    w_mod = din("w_mod", [D, 6 * D], F32)
    w_in = din("w_in", [D, NIN], F32)
    w_q_up = din("w_q_up", [384, 768], F32)
    w_kv_up = din("w_kv_up", [256, 1024], F32)
    w_o_attn = din("w_o_attn", [512, D], F32)
    w_fourier = din("w_fourier", [512, D], F32)
    w_out = din("w_out", [D, D], F32)
    w_router = din("w_router", [D, NE], F32)
    w_e_gate = din("w_e_gate", [NE, D, 512], F32)
    w_e_up = din("w_e_up", [NE, D, 512], F32)
    w_e_down = din("w_e_down", [NE, 512, D], F32)
    rope_cos = din("rope_cos", [128, NT, 32], F32)
    rope_sin = din("rope_sin", [128, NT, 32], F32)
    dft_c = din("dft_c", [T, T], BF16)
    dft_s = din("dft_s", [T, T], BF16)
    ch_c = din("ch_c", [128, 128], F32)
    ch_s = din("ch_s", [128, 128], F32)
    ident_b_d = din("ident_b", [128, 128], BF16)
    ident_f_d = din("ident_f", [128, 128], F32)
    iota_d = din("iota256", [128, 256], F32)
    tri_d = din("tri", [128, 128], BF16)
    rvc_d = din("rvc", [128, NT, NE, 2], BF16)
    out = nc.dram_tensor("out", [NB, T, D], F32, kind="ExternalOutput").ap()

    modr = dscr("modr", [5, 6 * D], F32)
    w_in_bf = dscr("w_in_bf", [D, NIN], BF16)
    wq_bf = dscr("wq_bf", [384, 768], BF16)
    wkv_bf = dscr("wkv_bf", [256, 1024], BF16)
    wo_bf = dscr("wo_bf", [512, D], BF16)
    wcs_bf = dscr("wcs_bf", [1024, D], BF16)
    wout_bf = dscr("wout_bf", [D, D], BF16)
    pqkvT = dscr("pqkvT", [NB, 5, 128, T], BF16)
    pkvcT = dscr("pkvcT", [NB, 2, 128, CTX], BF16)
    kpe_d = dscr("kpe_d", [NB, T + CTX, 32], F32)
    Ftm = dscr("Ftm", [NB, T, 512], BF16)
    gT = dscr("gT", [NB, 16, 128, T], BF16)
    qT_d = dscr("qT_d", [NB, H, QKD, T], BF16)
    kT_d = dscr("kT_d", [NB, H, QKD, T + CTX], BF16)
    vx_d = dscr("vx_d", [NB, T + CTX, H * 65], BF16)
    attnT = dscr("attnT", [NB, 4, 128, T], BF16)
    h2_d = dscr("h2_d", [NB, T, D], BF16)
    dbg_d = dscr("dbg_d", [128, 4096], F32)

    order = ["0", "A", "B", "C", "E", "F", "G"]
    stop_i = order.index(stop_after)

    def run(p):
        return order.index(p) <= stop_i and p not in skip

    with ExitStack() as gst:
        S = Sched(nc, gst)

        def sbg(name, shape, dt):
            return gst.enter_context(nc.sbuf_tensor(name, list(shape), dt))

        PSBIG = gst.enter_context(nc.psum_tensor("psbig", [128, 4096], F32))
        PS = [PSBIG[:, i * 512:(i + 1) * 512] for i in range(8)]
        PSB = bufs(8)

        ident_b = sbg("ident_b_s", [128, 128], BF16)
        ident_f = sbg("ident_f_s", [128, 128], F32)
        tbl = sbg("tbl", [128, 4, 5, 8], F32)
        a1 = sbg("a1", [128, 5, 8], F32)
        a2 = sbg("a2", [128, 5, 8], F32)
        n1g_s = sbg("n1g_s", [128, 8], F32)
        n2g_s = sbg("n2g_s", [128, 8], F32)
        epsb = sbg("epsb", [128, 1], F32)
        ones_b = sbg("ones_b", [128, 128], BF16)
        rstd_qkv = sbg("rstd_qkv", [128, NB, 2, NT + 2], F32)
        aff_all = sbg("aff_all", [128, NT, NB, NE], F32)
        idx_all = sbg("idx_all", [128, NB, NE, 2], I32)
        gate_all = sbg("gate_all", [128, NB, NE, 2], F32)
        Bconst, Btbl, Brstd, Baff, Bidx = Buf(), Buf(), Buf(), Buf(), Buf()

        def mm(out_ap, lhsT, rhs, start, stop, reads, writes, partial=()):
            S.op("pe", lambda e: e.matmul(out_ap, lhsT=lhsT, rhs=rhs, start=start, stop=stop),
                 reads=reads, writes=writes, partial=partial)

        def tr(out_ap, in_ap, ident, reads, writes=(), partial=()):
            S.op("pe", lambda e: e.transpose(out=out_ap, in_=in_ap, identity=ident),
                 reads=reads, writes=writes, partial=partial)

        def rstd_from_ss(dst, src, scale, reads, writes, tmp, tmpb):
            S.op("act", lambda e: e.activation(out=tmp, in_=src, func=AF.Ln, scale=scale, bias=epsb[:, 0:1]),
                 reads=list(reads) + [Bconst], writes=[tmpb])
            S.op("act", lambda e: e.activation(out=dst, in_=tmp, func=AF.Exp, scale=-0.5),
                 reads=[tmpb], partial=writes)

        with ExitStack() as st:
            def sb(name, shape, dt):
                return st.enter_context(nc.sbuf_tensor(name, list(shape), dt))

            S.dma("sp", lambda e: e.dma_start(out=ident_b[:], in_=ident_b_d), writes=[Bconst])
            S.dma("sp", lambda e: e.dma_start(out=ident_f[:], in_=ident_f_d), partial=[Bconst])
            S.dma("sp", lambda e: e.dma_start(out=n1g_s[:], in_=n1g), partial=[Bconst])
            S.dma("sp", lambda e: e.dma_start(out=n2g_s[:], in_=n2g), partial=[Bconst])
            S.op("dve", lambda e: e.memset(epsb[:], EPS), partial=[Bconst])
            S.op("dve", lambda e: e.memset(ones_b[:], 1.0), partial=[Bconst])

            cts = sb("cts", [128, 40], F32)
            sT = sb("sT", [128, 40], F32)
            et = sb("et", [128, 40], F32)
            Bc, BsT, Bet = Buf(), Buf(), Buf()
            S.dma("sp", lambda e: e.dma_start(out=cts[:], in_=cT), writes=[Bc])
            S.op("act", lambda e: e.activation(out=et[:], in_=cts[:], func=AF.Exp, scale=-1.0), reads=[Bc], writes=[Bet])
            S.op("dve", lambda e: e.tensor_scalar_add(out=et[:], in0=et[:], scalar1=1.0), reads=[Bet], writes=[Bet])
            S.op("dve", lambda e: e.reciprocal(out=et[:], in_=et[:]), reads=[Bet], writes=[Bet])
            S.op("dve", lambda e: e.tensor_mul(out=sT[:], in0=cts[:], in1=et[:]), reads=[Bc, Bet], writes=[BsT])

            stg0 = [sb(f"stg0{i}", [128, 4096], F32) for i in range(3)]
            Bstg0 = bufs(3)
            modrows = sb("modrows", [5, 6 * D], F32)
            bm = sb("bm", [5, 6 * D], F32)
            Bmr, Bbm = Buf(), Buf()
            S.dma("sp", lambda e: e.dma_start(out=bm[:], in_=bmod5), writes=[Bbm])
            for nb in range(12):
                k = nb % 3
                wv = stg0[k][:].rearrange("p (k n) -> p k n", k=8)
                S.dma("sp", lambda e, wv=wv, nb=nb: e.dma_start(
                    out=wv, in_=w_mod[:, nb * 512:(nb + 1) * 512].rearrange("(k p) n -> p k n", p=128)),
                    writes=[Bstg0[k]])
                pb = nb % 2
                for kc in range(8):
                    mm(PS[pb][0:5, :], sT[:, kc * 5:(kc + 1) * 5], wv[:, kc, :], kc == 0, kc == 7,
                       [BsT, Bstg0[k]], [PSB[pb]])
                S.op("dve", lambda e, nb=nb, pb=pb: e.tensor_tensor(
                    out=modrows[:, nb * 512:(nb + 1) * 512], in0=PS[pb][0:5, :], in1=bm[:, nb * 512:(nb + 1) * 512],
                    op=ALU.add), reads=[PSB[pb], Bbm], partial=[Bmr])
            Bmodr = Buf()
            S.dma("sp", lambda e: e.dma_start(out=modr, in_=modrows[:]), reads=[Bmr], writes=[Bmodr])
            for gi, grp in enumerate((0, 1, 3, 4)):
                for b in range(5):
                    S.dma("sp", lambda e, gi=gi, grp=grp, b=b: e.dma_start(
                        out=tbl[:, gi, b, :], in_=modr[b, grp * D:(grp + 1) * D].rearrange("(k p) -> p k", p=128),
                        allow_slow_non_contiguous=True), reads=[Bmodr], partial=[Btbl])
            for (av, gi, gs) in ((a1, 1, n1g_s), (a2, 3, n2g_s)):
                S.op("dve", lambda e, av=av, gi=gi: e.tensor_scalar_add(out=av[:], in0=tbl[:, gi, :, :], scalar1=1.0),
                     reads=[Btbl], partial=[Btbl])
                S.op("dve", lambda e, av=av, gs=gs: e.tensor_tensor(
                    out=av[:], in0=av[:], in1=gs[:].unsqueeze(1).to_broadcast([128, 5, 8]), op=ALU.mult),
                    reads=[Btbl, Bconst], partial=[Btbl])

            cb = [sb(f"cb{i}", [128, 4096], BF16) for i in range(2)]
            Bcb = bufs(2)
            Bw = {n: Buf() for n in ("w_in", "wq", "wkv", "wo", "wcs", "wout")}
            ctr = [0]

            def cast_store(src_ap, dst_ap, shape, wb, scale_ap=None):
                i = ctr[0]
                ctr[0] += 1
                k, c = i % 3, i % 2
                n = int(np.prod(shape[1:]))
                sv = stg0[k][:, 0:n]
                cv = cb[c][:, 0:n]
                if len(shape) == 3:
                    sv = sv.rearrange("p (a n) -> p a n", a=shape[1])
                    cv = cv.rearrange("p (a n) -> p a n", a=shape[1])
                S.dma("sp", lambda e: e.dma_start(out=sv, in_=src_ap), writes=[Bstg0[k]])
                if scale_ap is None:
                    eng = ("dve", "pool")[i % 2]
                    S.op(eng, lambda e: e.tensor_copy(out=cv, in_=sv), reads=[Bstg0[k]], writes=[Bcb[c]])
                else:
                    for a in range(shape[1]):
                        S.op("dve", lambda e, a=a: e.tensor_scalar(
                            out=cv[:, a, :], in0=sv[:, a, :], scalar1=scale_ap[:, a:a + 1], scalar2=None, op0=ALU.mult),
                            reads=[Bstg0[k], Bconst], partial=[Bcb[c]])
                S.dma("pool", lambda e: e.dma_start(out=dst_ap, in_=cv), reads=[Bcb[c]], partial=[wb])

            for kc in range(8):
                cast_store(w_in[kc * 128:(kc + 1) * 128, :], w_in_bf[kc * 128:(kc + 1) * 128, :], [128, NIN], Bw["w_in"])
            for k2 in range(2):
                cast_store(w_out[k2 * 512:(k2 + 1) * 512, :].rearrange("(k p) n -> p k n", p=128),
                           wout_bf[k2 * 512:(k2 + 1) * 512, :].rearrange("(k p) n -> p k n", p=128), [128, 4, D], Bw["wout"])
            cast_store(w_o_attn.rearrange("(k p) n -> p k n", p=128), wo_bf.rearrange("(k p) n -> p k n", p=128),
                       [128, 4, D], Bw["wo"])
            qag_s = sb("qag_s", [128, 3], F32)
            kvag_s = sb("kvag_s", [128, 2], F32)
            S.dma("sp", lambda e: e.dma_start(out=qag_s[:], in_=qag), partial=[Bconst])
            S.dma("sp", lambda e: e.dma_start(out=kvag_s[:], in_=kvag), partial=[Bconst])
            cast_store(w_q_up.rearrange("(k p) n -> p k n", p=128), wq_bf.rearrange("(k p) n -> p k n", p=128),
                       [128, 3, 768], Bw["wq"], scale_ap=qag_s)
            cast_store(w_kv_up.rearrange("(k p) n -> p k n", p=128), wkv_bf.rearrange("(k p) n -> p k n", p=128),
                       [128, 2, 1024], Bw["wkv"], scale_ap=kvag_s)
            wf = sb("wf", [128, 4, D], F32)
            chc = sb("chc", [128, 128], F32)
            chs = sb("chs", [128, 128], F32)
            wcs_s = sb("wcs_s", [128, 8, D], BF16)
            Bwf, Bwcs = Buf(), Buf()
            S.dma("sp", lambda e: e.dma_start(out=wf[:], in_=w_fourier.rearrange("(k p) n -> p k n", p=128)), writes=[Bwf])
            S.dma("sp", lambda e: e.dma_start(out=chc[:], in_=ch_c), partial=[Bwf])
            S.dma("sp", lambda e: e.dma_start(out=chs[:], in_=ch_s), partial=[Bwf])
            ii = 0
            for g in range(4):
                for (mi, msb, sgn) in ((0, chc, 1.0 / 512), (1, chs, -1.0 / 512)):
                    for nh in range(2):
                        pb = ii % 2
                        ii += 1
                        mm(PS[pb][:, :], msb[:], wf[:, g, nh * 512:(nh + 1) * 512], True, True, [Bwf], [PSB[pb]])
                        S.op("act", lambda e, pb=pb, g=g, mi=mi, nh=nh, sgn=sgn: e.activation(
                            out=wcs_s[:, mi * 4 + g, nh * 512:(nh + 1) * 512], in_=PS[pb][:, :], func=AF.Copy, scale=sgn),
                            reads=[PSB[pb]], partial=[Bwcs])
            S.dma("pool", lambda e: e.dma_start(out=wcs_bf.rearrange("(k p) n -> p k n", p=128), in_=wcs_s[:]),
                  reads=[Bwcs], writes=[Bw["wcs"]])
        S.barrier()

        ssq_all = sbg("ssq_all", [128, NB, 2, NT + 2], F32)
        Bssq = Buf()
        def phase_A():
            with ExitStack() as st:
                def sb(name, shape, dt):
                    return st.enter_context(nc.sbuf_tensor(name, list(shape), dt))

                NTILE_ALL = NB * (NT + 2)
                w_in_s = sb("w_in_s", [128, 8, NIN], BF16)
                Bwin = Buf()
                for kc in range(8):
                    S.dma("sp", lambda e, kc=kc: e.dma_start(out=w_in_s[:, kc, :], in_=w_in_bf[kc * 128:(kc + 1) * 128, :]),
                          partial=[Bwin])
                xt = [sb(f"xt{i}", [128, D], F32) for i in range(4)]
                xn = [sb(f"xn{i}", [128, D], BF16) for i in range(4)]
                hT = [sb(f"hT{i}", [128, 8, 512], BF16) for i in range(2)]
                junk = sb("junkA", [128, D], BF16)
                ssx = sb("ssx", [128, NTILE_ALL], F32)
                lnx = sb("lnx", [128, NTILE_ALL], F32)
                rsx = sb("rsx", [128, NTILE_ALL], F32)
                sq = [sb(f"sq{i}", [128, 5, 512], BF16) for i in range(2)]
                pq_s = [sb(f"pq_s{i}", [128, 5, 512], BF16) for i in range(2)]
                g_s = [sb(f"g_s{i}", [128, 4, 512], BF16) for i in range(2)]
                F_s = [sb(f"F_s{i}", [128, 512], BF16) for i in range(2)]
                kpe_s = [sb(f"kpe_s{i}", [128, 32], F32) for i in range(2)]
                Bxt, Bxn, BhT, Bpq, Bsq, Bgs, BFs, Bkpe = bufs(4), bufs(4), bufs(2), bufs(2), bufs(2), bufs(2), bufs(2), bufs(2)
                Bjunk, Bssx, Blnx, Brsx = Buf(), Buf(), Buf(), Buf()
                S.op("dve", lambda e: e.memset(ssx[:], 0.0), writes=[Bssx])
                S.op("dve", lambda e: e.memset(ssq_all[:], 1.0), writes=[Bssq])

                lat_chunks = ([("qk", c, OFF_Q + c * 128) for c in range(3)] + [("qk", 3 + c, OFF_KV + c * 128) for c in range(2)]
                              + [("g", c, OFF_GA + c * 128) for c in range(16)])
                ctx_chunks = [("qk", 3 + c, OFF_KV + c * 128) for c in range(2)]
                blocks = []
                col = 0
                for b in range(NB):
                    for tb in range(4):
                        blocks.append(dict(b=b, src=x4[b, tb * 512:(tb + 1) * 512, :], ntile=4, tb=tb, chunks=lat_chunks,
                                           cols=(tb * 512, (tb + 1) * 512), ctx=False, tile0=tb * 4, col0=col))
                        col += 4
                    blocks.append(dict(b=b, src=ctx4[b], ntile=2, tb=0, chunks=ctx_chunks, cols=(0, CTX), ctx=True, tile0=NT, col0=col))
                    col += 2

                pi = 0
                for blk in blocks:
                    for i in range(blk["ntile"]):
                        k = pi % 4
                        pi += 1
                        cidx = blk["col0"] + i
                        S.dma("sp", lambda e, k=k, i=i, blk=blk: e.dma_start(out=xt[k][:], in_=blk["src"][i * 128:(i + 1) * 128, :]),
                              writes=[Bxt[k]])
                        S.op("act", lambda e, k=k, cidx=cidx: e.activation(
                            out=junk[:], in_=xt[k][:], func=AF.Square, accum_out=ssx[:, cidx:cidx + 1]),
                            reads=[Bxt[k], Bssx], writes=[Bjunk], partial=[Bssx])
                S.op("act", lambda e: e.activation(out=lnx[:], in_=ssx[:], func=AF.Ln, scale=1.0 / D, bias=epsb[:, 0:1]),
                     reads=[Bssx, Bconst], writes=[Blnx])
                S.op("act", lambda e: e.activation(out=rsx[:], in_=lnx[:], func=AF.Exp, scale=-0.5), reads=[Blnx], writes=[Brsx])

                cnt = {"fm": 0, "g": 0, "t": 0}

                def P1(n):
                    if n >= len(blocks):
                        return
                    blk = blocks[n]
                    for i in range(blk["ntile"]):
                        cidx = blk["col0"] + i
                        S.dma("sp", lambda e, i=i: e.dma_start(out=xt[i][:], in_=blk["src"][i * 128:(i + 1) * 128, :]), writes=[Bxt[i]])
                        S.op("dve", lambda e, i=i, cidx=cidx: e.tensor_scalar(
                            out=xn[i][:], in0=xt[i][:], scalar1=rsx[:, cidx:cidx + 1], scalar2=None, op0=ALU.mult),
                            reads=[Bxt[i], Brsx], writes=[Bxn[i]])

                def P2_thunks(n):
                    if n >= len(blocks):
                        return []
                    blk = blocks[n]
                    bi = n % 2
                    b = blk["b"]
                    bsel = 4 if blk["ctx"] else b
                    th = []
                    for i in range(blk["ntile"]):
                        def p2a(i=i):
                            ti = cnt["t"]
                            cnt["t"] += 1
                            pt = 6 + (ti % 2)
                            ptv = PS[pt][:].bitcast(BF16).rearrange("p (k t) -> p k t", k=8)
                            for kc in range(8):
                                tr(ptv[:, kc, :], xn[i][:, kc * 128:(kc + 1) * 128], ident_b[:], [Bxn[i], Bconst],
                                   writes=[PSB[pt]] if kc == 0 else (), partial=() if kc == 0 else [PSB[pt]])
                            hv = hT[bi][:, :, i * 128:(i + 1) * 128]
                            S.op("dve", lambda e: e.tensor_tensor(
                                out=hv, in0=ptv, in1=a1[:, bsel, :].unsqueeze(2).to_broadcast([128, 8, 128]), op=ALU.mult),
                                reads=[PSB[pt], Btbl], partial=[BhT[bi]])
                            S.op("dve", lambda e: e.tensor_tensor(
                                out=hv, in0=hv, in1=tbl[:, 0, bsel, :].unsqueeze(2).to_broadcast([128, 8, 128]), op=ALU.add),
                                reads=[Btbl, BhT[bi]], partial=[BhT[bi]])

                        def p2b(i=i):
                            k = i % 2
                            if not blk["ctx"]:
                                pf = i % 2
                                for kc in range(8):
                                    mm(PS[pf][:, :], hT[bi][:, kc, i * 128:(i + 1) * 128], w_in_s[:, kc, OFF_F:OFF_F + 512],
                                       kc == 0, kc == 7, [BhT[bi], Bwin], [PSB[pf]])
                                S.op("act", lambda e: e.activation(out=F_s[k][:], in_=PS[pf][:, :], func=AF.Copy),
                                     reads=[PSB[pf]], writes=[BFs[k]])
                                r0 = blk["tb"] * 512 + i * 128
                                S.dma("pool", lambda e: e.dma_start(out=Ftm[b, r0:r0 + 128, :], in_=F_s[k][:]), reads=[BFs[k]])
                            for kc in range(8):
                                mm(PS[2][:, 0:32], hT[bi][:, kc, i * 128:(i + 1) * 128], w_in_s[:, kc, OFF_KPE:OFF_KPE + 32],
                                   kc == 0, kc == 7, [BhT[bi], Bwin], [PSB[2]])
                            S.op("dve", lambda e: e.tensor_copy(out=kpe_s[k][:], in_=PS[2][:, 0:32]), reads=[PSB[2]], writes=[Bkpe[k]])
                            r1 = (blk["tile0"] + i) * 128
                            S.dma("pool", lambda e: e.dma_start(out=kpe_d[b, r1:r1 + 128, :], in_=kpe_s[k][:]), reads=[Bkpe[k]])
                        th.append(p2a)
                        th.append(p2b)
                    return th

                def M_thunks(n):
                    if n < 0 or n >= len(blocks):
                        return []
                    blk = blocks[n]
                    bi = n % 2
                    b = blk["b"]
                    ntile = blk["ntile"]
                    ntok = ntile * 128
                    cols = blk["cols"]
                    th = []
                    for (kind, cidx, col0) in blk["chunks"]:
                        def chunk(kind=kind, cidx=cidx, col0=col0):
                            pf = 3 + (cnt["fm"] % 2)
                            cnt["fm"] += 1
                            for kc in range(8):
                                mm(PS[pf][:, 0:ntok], w_in_s[:, kc, col0:col0 + 128], hT[bi][:, kc, 0:ntok],
                                   kc == 0, kc == 7, [BhT[bi], Bwin], [PSB[pf]])
                            if kind == "qk":
                                S.op("act", lambda e: e.activation(out=pq_s[bi][:, cidx, 0:ntok], in_=PS[pf][:, 0:ntok], func=AF.Copy),
                                     reads=[PSB[pf]], partial=[Bpq[bi]])
                                S.op("act", lambda e: e.activation(out=sq[bi][:, cidx, 0:ntok], in_=PS[pf][:, 0:ntok], func=AF.Square),
                                     reads=[PSB[pf]], partial=[Bsq[bi]])
                            else:
                                gi = cnt["g"]
                                cnt["g"] += 1
                                gs_i = (gi // 4) % 2
                                S.op("act", lambda e: e.activation(out=g_s[gs_i][:, gi % 4, :], in_=PS[pf][:, :], func=AF.Sigmoid),
                                     reads=[PSB[pf]], partial=[Bgs[gs_i]])
                                if gi % 4 == 3:
                                    c0 = cidx - 3
                                    S.dma("pool", lambda e: e.dma_start(
                                        out=gT[b, c0:c0 + 4, :, cols[0]:cols[1]].rearrange("c p t -> p c t"), in_=g_s[gs_i][:]),
                                        reads=[Bgs[gs_i]])
                        th.append(chunk)

                    def stats():
                        groups = [(0, 3, 0), (3, 5, 1)] if not blk["ctx"] else [(3, 5, 1)]
                        for (c0, c1, which) in groups:
                            for i in range(ntile):
                                for c in range(c0, c1):
                                    mm(PS[5][:, i:i + 1], sq[bi][:, c, i * 128:(i + 1) * 128], ones_b[:, 0:1],
                                       c == c0, c == c1 - 1, [Bsq[bi], Bconst], [PSB[5]])
                            t0 = blk["tile0"]
                            S.op("dve", lambda e, which=which: e.tensor_copy(out=ssq_all[:, b, which, t0:t0 + ntile], in_=PS[5][:, 0:ntile]),
                                 reads=[PSB[5]], partial=[Bssq])
                        if not blk["ctx"]:
                            S.dma("pool", lambda e: e.dma_start(
                                out=pqkvT[b, :, :, cols[0]:cols[1]].rearrange("c p t -> p c t"), in_=pq_s[bi][:]), reads=[Bpq[bi]])
                        else:
                            S.dma("pool", lambda e: e.dma_start(
                                out=pkvcT[b].rearrange("c p t -> p c t"), in_=pq_s[bi][:, 3:5, 0:CTX]), reads=[Bpq[bi]])
                    th.append(stats)
                    return th

                def interleave(a, bq):
                    na, nb_ = len(a), len(bq)
                    j = 0
                    for i, f in enumerate(a):
                        f()
                        while j < nb_ and (j + 1) * na <= (i + 1) * nb_ * 1.0001 + 1e-9:
                            bq[j]()
                            j += 1
                    while j < nb_:
                        bq[j]()
                        j += 1

                P1(0)
                for f in P2_thunks(0):
                    f()
                for n in range(len(blocks)):
                    P1(n + 1)
                    interleave(M_thunks(n), P2_thunks(n + 1))
                lnq = sb("lnq", [128, NB, 2, NT + 2], F32)
                Blnq = Buf()
                for which, nfeat in ((0, 384), (1, 256)):
                    S.op("act", lambda e, which=which, nfeat=nfeat: e.activation(
                        out=lnq[:, :, which, :], in_=ssq_all[:, :, which, :], func=AF.Ln, scale=1.0 / nfeat, bias=epsb[:, 0:1]),
                        reads=[Bssq, Bconst], partial=[Blnq])
                S.op("act", lambda e: e.activation(out=rstd_qkv[:], in_=lnq[:], func=AF.Exp, scale=-0.5), reads=[Blnq], writes=[Brstd])
            S.barrier()

        if run("A"):
            phase_A()

        def phase_B():
            with ExitStack() as st:
                def sb(name, shape, dt):
                    return st.enter_context(nc.sbuf_tensor(name, list(shape), dt))

                wq_s = sb("wq_s", [128, 3, 768], BF16)
                wkv_s = sb("wkv_s", [128, 2, 1024], BF16)
                qg_s = sb("qg_s", [128, QKD], F32)
                kg_s = sb("kg_s", [128, QKD], F32)
                cos_s = sb("cos_s", [128, NT, 32], F32)
                sin_s = sb("sin_s", [128, NT, 32], F32)
                BwB = Buf()
                S.dma("sp", lambda e: e.dma_start(out=wq_s[:], in_=wq_bf.rearrange("(k p) n -> p k n", p=128)), partial=[BwB])
                S.dma("sp", lambda e: e.dma_start(out=wkv_s[:], in_=wkv_bf.rearrange("(k p) n -> p k n", p=128)), partial=[BwB])
                S.dma("sp", lambda e: e.dma_start(out=qg_s[:], in_=qg_bc), partial=[BwB])
                S.dma("sp", lambda e: e.dma_start(out=kg_s[:], in_=kg_bc), partial=[BwB])
                S.dma("sp", lambda e: e.dma_start(out=cos_s[:], in_=rope_cos), partial=[BwB])
                S.dma("sp", lambda e: e.dma_start(out=sin_s[:], in_=rope_sin), partial=[BwB])
                S.op("dve", lambda e: e.tensor_scalar(out=qg_s[:], in0=qg_s[:], scalar1=float(QKD ** -0.5), scalar2=None,
                                                       op0=ALU.mult), reads=[BwB], writes=[BwB])
                pq_in = [sb(f"pq_in{i}", [128, 3, T], BF16) for i in range(2)]
                pkv_in = [sb(f"pkv_in{i}", [128, 2, T + CTX], BF16) for i in range(2)]
                kpe_in = [sb(f"kpe_in{i}", [128, NT + 2, 32], F32) for i in range(2)]
                qT_all = sb("qT_all", [128, H, T], BF16)
                kT_all = sb("kT_all", [128, H, T + CTX], BF16)
                Bpqin, Bpkvin, Bkpein = bufs(2), bufs(2), bufs(2)
                BqT, BkT = Buf(), Buf()
                W = 5
                xf = [sb(f"xf{i}", [128, H, QKD], F32) for i in range(W)]
                sqf = [sb(f"sqf{i}", [128, H, QKD], F32) for i in range(W)]
                ssh = [sb(f"ssh{i}", [128, H], F32) for i in range(W)]
                lnh = [sb(f"lnh{i}", [128, H], F32) for i in range(W)]
                rsh = [sb(f"rsh{i}", [128, H], F32) for i in range(W)]
                t1 = [sb(f"t1_{i}", [128, H, 32], F32) for i in range(W)]
                t2 = [sb(f"t2_{i}", [128, H, 32], F32) for i in range(W)]
                xb = [sb(f"xb{i}", [128, H, QKD], BF16) for i in range(W)]
                vx = [sb(f"vx{i}", [128, H, 65], BF16) for i in range(W)]
                Bxf, Bsqf, Bssh, Blnh, Brsh, Bt1, Bt2, Bxb, Bvx = (bufs(W) for _ in range(9))
                for i in range(W):
                    S.op("pool", lambda e, i=i: e.memset(vx[i][:], 1.0), writes=[Bvx[i]])
                pc = {"n": 0, "b": 0}

                def task(kind, b, ib, i, k):
                    bp = pc["b"] % 3
                    pc["b"] += 1
                    banks = (2 * bp, 2 * bp + 1)
                    xv = xf[k]
                    if kind == "k":
                        width, nk, gain, dstT, dBuf = 128, 64, kg_s, kT_all, BkT
                        rs_ap = rstd_qkv[:, b, 1, i:i + 1]
                        rope_tile = i if i < NT else None
                        for nh in range(2):
                            for c in range(2):
                                mm(PS[banks[nh]][:, :], pkv_in[ib][:, c, i * 128:(i + 1) * 128], wkv_s[:, c, nh * 512:(nh + 1) * 512],
                                   c == 0, c == 1, [Bpkvin[ib], BwB], [PSB[banks[nh]]])
                    else:
                        width, nk, gain, dstT, dBuf = QKD, QKD, qg_s, qT_all, BqT
                        rs_ap = rstd_qkv[:, b, 0, i:i + 1]
                        rope_tile = i
                        for nh in range(2):
                            for c in range(3):
                                mm(PS[banks[nh]][:, 0:384], pq_in[ib][:, c, i * 128:(i + 1) * 128], wq_s[:, c, nh * 384:(nh + 1) * 384],
                                   c == 0, c == 2, [Bpqin[ib], BwB], [PSB[banks[nh]]])
                    for nh in range(2):
                        pv = PS[banks[nh]][:, 0:4 * width].rearrange("p (h w) -> p h w", h=4)
                        S.op("act", lambda e, pv=pv, nh=nh: e.activation(
                            out=xv[:, nh * 4:(nh + 1) * 4, 0:nk], in_=pv[:, :, 0:nk], func=AF.Copy, scale=rs_ap),
                            reads=[PSB[banks[nh]], Brstd], partial=[Bxf[k]])
                        if kind == "k":
                            S.op("act", lambda e, pv=pv, nh=nh: e.activation(
                                out=vx[k][:, nh * 4:(nh + 1) * 4, 0:64], in_=pv[:, :, 64:128], func=AF.Copy, scale=rs_ap),
                                reads=[PSB[banks[nh]], Brstd], partial=[Bvx[k]])
                    yield
                    if kind == "k":
                        kpe_ap = kpe_in[ib][:, i, :]
                        S.op("pool", lambda e: e.tensor_copy(out=xv[:, :, 64:96], in_=kpe_ap.unsqueeze(1).to_broadcast([128, H, 32])),
                             reads=[Bkpein[ib]], partial=[Bxf[k]])
                        S.dma("sp", lambda e: e.dma_start(
                            out=vx_d[b, i * 128:(i + 1) * 128, :], in_=vx[k][:].rearrange("p h w -> p (h w)")), reads=[Bvx[k]])
                        yield
                    S.op("pool", lambda e: e.tensor_tensor(out=sqf[k][:], in0=xv[:], in1=xv[:], op=ALU.mult),
                         reads=[Bxf[k]], writes=[Bsqf[k]])
                    yield
                    S.op("dve", lambda e: e.tensor_reduce(out=ssh[k][:], in_=sqf[k][:], axis=AX.X, op=ALU.add),
                         reads=[Bsqf[k]], writes=[Bssh[k]])
                    yield
                    S.op("act", lambda e: e.activation(out=lnh[k][:], in_=ssh[k][:], func=AF.Ln, scale=1.0 / QKD, bias=epsb[:, 0:1]),
                         reads=[Bssh[k], Bconst], writes=[Blnh[k]])
                    yield
                    S.op("act", lambda e: e.activation(out=rsh[k][:], in_=lnh[k][:], func=AF.Exp, scale=-0.5),
                         reads=[Blnh[k]], writes=[Brsh[k]])
                    yield
                    S.op("dve", lambda e: e.tensor_tensor(out=xv[:], in0=xv[:], in1=rsh[k][:].unsqueeze(2).to_broadcast([128, H, QKD]),
                                                           op=ALU.mult), reads=[Brsh[k]], writes=[Bxf[k]])
                    yield
                    S.op("dve", lambda e: e.tensor_tensor(out=xv[:], in0=xv[:], in1=gain[:].unsqueeze(1).to_broadcast([128, H, QKD]),
                                                           op=ALU.mult), reads=[BwB], writes=[Bxf[k]])
                    yield
                    S.op("act", lambda e: e.activation(out=xb[k][:, :, 0:64], in_=xv[:, :, 0:64], func=AF.Copy),
                         reads=[Bxf[k]], partial=[Bxb[k]])
                    if rope_tile is not None:
                        pe4 = xv[:, :, 64:96].rearrange("p h (g u) -> p h g u", g=2)
                        cs = cos_s[:, rope_tile, :].unsqueeze(1).to_broadcast([128, H, 32])
                        sn4 = sin_s[:, rope_tile, :].rearrange("p (g u) -> p g u", g=2)
                        t24 = t2[k][:].rearrange("p h (g u) -> p h g u", g=2)
                        S.op("dve", lambda e: e.tensor_tensor(out=t1[k][:], in0=xv[:, :, 64:96], in1=cs, op=ALU.mult),
                             reads=[Bxf[k], BwB], writes=[Bt1[k]])
                        S.op("pool", lambda e: e.tensor_tensor(
                            out=t24[:, :, :, 0:8], in0=pe4[:, :, :, 8:16],
                            in1=sn4[:, :, 0:8].unsqueeze(1).to_broadcast([128, H, 2, 8]), op=ALU.mult),
                            reads=[Bxf[k], BwB], writes=[Bt2[k]])
                        yield
                        S.op("pool", lambda e: e.tensor_tensor(
                            out=t24[:, :, :, 8:16], in0=pe4[:, :, :, 0:8],
                            in1=sn4[:, :, 8:16].unsqueeze(1).to_broadcast([128, H, 2, 8]), op=ALU.mult),
                            reads=[Bxf[k], BwB], partial=[Bt2[k]])
                        yield
                        S.op("dve", lambda e: e.tensor_tensor(out=xb[k][:, :, 64:96], in0=t1[k][:], in1=t2[k][:], op=ALU.add),
                             reads=[Bt1[k], Bt2[k]], partial=[Bxb[k]])
                    else:
                        S.op("dve", lambda e: e.tensor_copy(out=xb[k][:, :, 64:96], in_=xv[:, :, 64:96]),
                             reads=[Bxf[k]], partial=[Bxb[k]])
                    yield
                    pt = 6 + (pc["n"] % 2)
                    pc["n"] += 1
                    ptv = PS[pt][:].bitcast(BF16).rearrange("p (h t) -> p h t", h=H)
                    for h in range(H):
                        tr(ptv[0:QKD, h, :], xb[k][:, h, :], ident_b[:], [Bxb[k], Bconst],
                           writes=[PSB[pt]] if h == 0 else (), partial=() if h == 0 else [PSB[pt]])
                    col0 = i * 128
                    S.op("act", lambda e: e.activation(out=dstT[0:QKD, :, col0:col0 + 128], in_=ptv[0:QKD, :, :], func=AF.Copy),
                         reads=[PSB[pt]], partial=[dBuf])

                def run_window(tasks):
                    slots = [None] * W
                    ti = 0
                    while True:
                        for k in range(W):
                            if slots[k] is None and ti < len(tasks):
                                kind, b, ib, i = tasks[ti]
                                ti += 1
                                slots[k] = task(kind, b, ib, i, k)
                        if all(sl is None for sl in slots):
                            break
                        for k in range(W):
                            if slots[k] is not None:
                                try:
                                    next(slots[k])
                                except StopIteration:
                                    slots[k] = None

                def load_b(b):
                    if b >= NB:
                        return
                    ib = b % 2
                    S.dma("sp", lambda e: e.dma_start(out=pq_in[ib][:], in_=pqkvT[b, 0:3].rearrange("c p t -> p c t")),
                          writes=[Bpqin[ib]])
                    S.dma("sp", lambda e: e.dma_start(out=pkv_in[ib][:, :, 0:T], in_=pqkvT[b, 3:5].rearrange("c p t -> p c t")),
                          writes=[Bpkvin[ib]])
                    S.dma("sp", lambda e: e.dma_start(out=pkv_in[ib][:, :, T:T + CTX], in_=pkvcT[b].rearrange("c p t -> p c t")),
                          partial=[Bpkvin[ib]])
                    S.dma("sp", lambda e: e.dma_start(out=kpe_in[ib][:], in_=kpe_d[b].rearrange("(i p) f -> p i f", p=128)),
                          writes=[Bkpein[ib]])

                load_b(0)
                for b in range(NB):
                    ib = b % 2
                    load_b(b + 1)
                    tasks = []
                    for i in range(NT + 2):
                        tasks.append(("k", b, ib, i))
                        if i < NT:
                            tasks.append(("q", b, ib, i))
                    run_window(tasks)
                    S.dma("sp", lambda e, b=b: e.dma_start(out=qT_d[b].rearrange("h d t -> d h t"), in_=qT_all[0:QKD]),
                          reads=[BqT])
                    S.dma("sp", lambda e, b=b: e.dma_start(out=kT_d[b].rearrange("h d t -> d h t"), in_=kT_all[0:QKD]),
                          reads=[BkT])
            S.barrier()

        if run("B"):
            phase_B()

        def phase_C():
            with ExitStack() as st:
                def sb(name, shape, dt):
                    return st.enter_context(nc.sbuf_tensor(name, list(shape), dt))

                NKC = (T + CTX) // 128
                qTs = [sb(f"qTs{i}", [128, T], BF16) for i in range(3)]
                kTs = [sb(f"kTs{i}", [128, T + CTX], BF16) for i in range(3)]
                vxs = [sb(f"vxs{i}", [128, NKC, H * 65], BF16) for i in range(2)]
                pT = [sb(f"pT{i}", [128, 1024], BF16) for i in range(3)]
                attn_s = sb("attn_s", [128, NT, 512], BF16)
                attnT_s = sb("attnT_s", [128, 4, T], BF16)
                rsum = sb("rsum", [128, 4], F32)
                Bq, Bk, Bv, BpT = bufs(3), bufs(3), bufs(2), bufs(3)
                Battn, BattnT, Brsum = Buf(), Buf(), Buf()
                for i in range(3):
                    S.op("pool", lambda e, i=i: e.memset(qTs[i][:], 0.0), writes=[Bq[i]])
                    S.op("pool", lambda e, i=i: e.memset(kTs[i][:], 0.0), writes=[Bk[i]])
                items = [(b, h, qb, kp) for b in range(NB) for h in range(H) for qb in range(4) for kp in range(NKC // 2)]
                heads = [(b, h) for b in range(NB) for h in range(H)]

                def load_head(n):
                    if n >= len(heads):
                        return
                    b, h = heads[n]
                    ih = n % 3
                    S.dma("sp", lambda e: e.dma_start(out=qTs[ih][0:QKD, :], in_=qT_d[b, h]), writes=[Bq[ih]])
                    S.dma("sp", lambda e: e.dma_start(out=kTs[ih][0:QKD, :], in_=kT_d[b, h]), writes=[Bk[ih]])

                def load_v(b):
                    if b >= NB:
                        return
                    iv = b % 2
                    S.dma("sp", lambda e: e.dma_start(out=vxs[iv][:], in_=vx_d[b].rearrange("(c p) w -> p c w", p=128)),
                          writes=[Bv[iv]])

                load_v(0)
                load_head(0)

                def st_qk(n):
                    b, h, qb, kp = items[n]
                    hn = b * H + h
                    ih = hn % 3
                    if qb == 0 and kp == 0:
                        load_head(hn + 1)
                        if h == 0:
                            load_v(b + 1)
                    pr = n % 2
                    for u in range(2):
                        kc = 2 * kp + u
                        mm(PS[2 * pr + u], kTs[ih][:, kc * 128:(kc + 1) * 128], qTs[ih][:, qb * 512:(qb + 1) * 512],
                           True, True, [Bq[ih], Bk[ih]], [PSB[2 * pr]] if u == 0 else (), () if u == 0 else [PSB[2 * pr]])

                def st_exp(n):
                    pr, ip = n % 2, n % 3
                    S.op("act", lambda e: e.activation(out=pT[ip][:], in_=PSBIG[:, pr * 1024:(pr + 1) * 1024], func=AF.Exp),
                         reads=[PSB[2 * pr]], writes=[BpT[ip]])

                def st_pv(n):
                    b, h, qb, kp = items[n]
                    ip, iv = n % 3, b % 2
                    for u in range(2):
                        kc = 2 * kp + u
                        for j in range(4):
                            mm(PS[4 + j][:, 0:65], pT[ip][:, u * 512 + j * 128:u * 512 + (j + 1) * 128],
                               vxs[iv][:, kc, h * 65:(h + 1) * 65], kc == 0, kc == NKC - 1, [BpT[ip], Bv[iv]], [PSB[4 + j]])
                    if kp != NKC // 2 - 1:
                        return
                    ov = PSBIG[:, 2048:4096].rearrange("p (j c) -> p j c", j=4)
                    S.op("dve", lambda e: e.reciprocal(out=rsum[:], in_=ov[:, :, 64]),
                         reads=[PSB[4], PSB[5], PSB[6], PSB[7]], writes=[Brsum])
                    S.op("dve", lambda e: e.tensor_tensor(
                        out=attn_s[:, qb * 4:(qb + 1) * 4, h * 64:(h + 1) * 64], in0=ov[:, :, 0:64],
                        in1=rsum[:].unsqueeze(2).to_broadcast([128, 4, 64]), op=ALU.mult),
                        reads=[PSB[4], PSB[5], PSB[6], PSB[7], Brsum], partial=[Battn])
                    if not (h == H - 1 and qb == 3):
                        return
                    for i in range(NT):
                        pt = 4 + (i % 4)
                        ptv = PS[pt][:].bitcast(BF16).rearrange("p (c t) -> p c t", c=8)
                        for c in range(4):
                            tr(ptv[:, c, :], attn_s[:, i, c * 128:(c + 1) * 128], ident_b[:], [Battn, Bconst],
                               writes=[PSB[pt]] if c == 0 else (), partial=() if c == 0 else [PSB[pt]])
                        S.op("act", lambda e, ptv=ptv, i=i: e.activation(out=attnT_s[:, :, i * 128:(i + 1) * 128], in_=ptv[:, 0:4, :],
                                                                            func=AF.Copy), reads=[PSB[pt]], partial=[BattnT])
                    S.dma("pool", lambda e: e.dma_start(out=attnT[b].rearrange("c p t -> p c t"), in_=attnT_s[:]),
                          reads=[BattnT])

                nit = len(items)
                for s_ in range(nit + 2):
                    if s_ < nit:
                        st_qk(s_)
                    if 0 <= s_ - 1 < nit:
                        st_exp(s_ - 1)
                    if 0 <= s_ - 2 < nit:
                        st_pv(s_ - 2)
            S.barrier()

        if run("C"):
            phase_C()

        def phase_E():
            with ExitStack() as st:
                def sb(name, shape, dt):
                    return st.enter_context(nc.sbuf_tensor(name, list(shape), dt))

                wo_s = sb("wo_s", [128, 4, D], BF16)
                wcs_s2 = sb("wcs_s2", [128, 8, D], BF16)
                wout_s = sb("wout_s", [128, 8, D], BF16)
                wr_f = sb("wr_f", [128, 8, NE], F32)
                wr_s = sb("wr_s", [128, 8, NE], BF16)
                n2bc = sb("n2bc", [128, D], F32)
                BwE = Buf()
                S.dma("sp", lambda e: e.dma_start(out=wo_s[:], in_=wo_bf.rearrange("(k p) n -> p k n", p=128)), partial=[BwE])
                S.dma("sp", lambda e: e.dma_start(out=wcs_s2[:], in_=wcs_bf.rearrange("(k p) n -> p k n", p=128)), partial=[BwE])
                S.dma("sp", lambda e: e.dma_start(out=wout_s[:], in_=wout_bf.rearrange("(k p) n -> p k n", p=128)), partial=[BwE])
                S.dma("sp", lambda e: e.dma_start(out=wr_f[:], in_=w_router.rearrange("(k p) n -> p k n", p=128)), partial=[BwE])
                S.dma("sp", lambda e: e.dma_start(out=n2bc[:], in_=n2g_bc), partial=[BwE])
                S.op("dve", lambda e: e.tensor_copy(out=wr_s[:], in_=wr_f[:]), reads=[BwE], partial=[BwE])
                Fs = [sb("Fs0", [128, NT, 512], BF16)] * 2
                dblk = [sb(f"dblk{i}", [128, NT, 512], BF16) for i in range(2)]
                uvt = [sb("uvt0", [128, 8, 512], BF16)] * 2
                gblk = [sb("gblk0", [128, 16, 512], BF16)] * 2
                ablk = [sb(f"ablk{i}", [128, 4, 512], BF16) for i in range(2)]
                mT = [sb("mT0", [128, 8, 512], BF16)] * 2
                tA = [sb(f"tA{i}", [128, 512], F32) for i in range(2)]
                tB = [sb(f"tB{i}", [128, 512], F32) for i in range(2)]
                g1bc = [sb(f"g1bc{i}", [128, D], F32) for i in range(2)]
                a2bc = [sb(f"a2bc{i}", [128, D], F32) for i in range(2)]
                s2bc = [sb(f"s2bc{i}", [128, D], F32) for i in range(2)]
                xt = [sb(f"xtE{i}", [128, D], F32) for i in range(2)]
                x1 = [sb(f"x1_{i}", [128, D], F32) for i in range(2)]
                tt = sb("ttE", [128, D], F32)
                tt2 = sb("ttE2", [128, D], F32)
                junk = sb("junkE", [128, D], BF16)
                h2s = [sb(f"h2s{i}", [128, D], BF16) for i in range(2)]
                h2T = [sb(f"h2T{i}", [128, 8, 128], BF16) for i in range(2)]
                ss2 = sb("ss2", [128, NB * NT], F32)
                ln2 = sb("ln2", [128, NB * NT], F32)
                rs2 = sb("rs2", [128, NB * NT], F32)
                mx = sb("mx", [128, 2], F32)
                ex = sb("ex", [128, NE], F32)
                sm = sb("sm", [128, 2], F32)
                BFs2, Bd, Buv, Bg, Ba, BmT, BtA, BtB = [Buf()] * 2, bufs(2), [Buf()] * 2, [Buf()] * 2, bufs(2), [Buf()] * 2, bufs(2), bufs(2)
                Bg1, Ba2, Bs2, BxtE, Bx1, Bh2s, Bh2T = bufs(2), bufs(2), bufs(2), bufs(2), bufs(2), bufs(2), bufs(2)
                Btt, BjE, Bss2, Bln2, Brs2, Bmx, Bex, Bsm, Btt2 = Buf(), Buf(), Buf(), Buf(), Buf(), Buf(), Buf(), Buf(), Buf()
                S.op("dve", lambda e: e.memset(ss2[:], 0.0), writes=[Bss2])
                S.op("dve", lambda e: e.memset(sm[:], 0.0), writes=[Bsm])
                cE = {"d": 0, "p": 0, "t": 0}

                def load_batch(b):
                    ib = b % 2
                    S.dma("sp", lambda e: e.dma_start(out=Fs[ib][:], in_=Ftm[b].rearrange("(c p) f -> p c f", p=128)),
                          writes=[BFs2[ib]])
                    S.dma("sp", lambda e: e.dma_start(out=g1bc[ib][:], in_=modr[b, 2 * D:3 * D].partition_broadcast(128)),
                          writes=[Bg1[ib]])
                    S.dma("sp", lambda e: e.dma_start(out=a2bc[ib][:], in_=modr[b, 4 * D:5 * D].partition_broadcast(128)),
                          writes=[Ba2[ib]])
                    S.dma("sp", lambda e: e.dma_start(out=s2bc[ib][:], in_=modr[b, 3 * D:4 * D].partition_broadcast(128)),
                          writes=[Bs2[ib]])
                    S.op("dve", lambda e: e.tensor_scalar_add(out=a2bc[ib][:], in0=a2bc[ib][:], scalar1=1.0),
                         reads=[Ba2[ib]], writes=[Ba2[ib]])
                    S.op("dve", lambda e: e.tensor_mul(out=a2bc[ib][:], in0=a2bc[ib][:], in1=n2bc[:]),
                         reads=[Ba2[ib], BwE], writes=[Ba2[ib]])

                def fourier(g):
                    b, tb = divmod(g, 4)
                    ib, bi = b % 2, g % 2
                    cols = (tb * 512, (tb + 1) * 512)
                    if tb == 0:
                        load_batch(b)
                    S.dma("sp", lambda e: e.dma_start(
                        out=gblk[bi][:], in_=gT[b, :, :, cols[0]:cols[1]].rearrange("c p t -> p c t")), writes=[Bg[bi]])
                    S.dma("sp", lambda e: e.dma_start(
                        out=ablk[bi][:], in_=attnT[b, :, :, cols[0]:cols[1]].rearrange("c p t -> p c t")), writes=[Ba[bi]])
                    for mi in range(2):
                        di = mi
                        for fc in range(4):
                            pb = cE["p"] % 2
                            cE["p"] += 1
                            for c in range(NT):
                                mm(PS[pb][:, :], Fs[ib][:, c, fc * 128:(fc + 1) * 128], dblk[di][:, c, :], c == 0, c == NT - 1,
                                   [BFs2[ib], Bd[di]], [PSB[pb]])
                            S.op("act", lambda e, pb=pb, mi=mi, fc=fc: e.activation(
                                out=uvt[bi][:, mi * 4 + fc, :], in_=PS[pb][:, :], func=AF.Copy), reads=[PSB[pb]], partial=[Buv[bi]])
                            if fc == 3:
                                load_dft(g + 1, mi)
                            yield

                def load_dft(g, mi):
                    if g >= NB * 4:
                        return
                    tb = g % 4
                    dm = (dft_c, dft_s)[mi]
                    S.dma("sp", lambda e: e.dma_start(
                        out=dblk[mi][:], in_=dm[:, tb * 512:(tb + 1) * 512].rearrange("(c p) t -> p c t", p=128)), writes=[Bd[mi]])

                def merge(g):
                    bi = g % 2
                    for n_ in range(8):
                        pa, pf = 2 + (n_ % 2) * 2, 3 + (n_ % 2) * 2
                        for kc in range(4):
                            mm(PS[pa][:, :], wo_s[:, kc, n_ * 128:(n_ + 1) * 128], ablk[bi][:, kc, :], kc == 0, kc == 3,
                               [BwE, Ba[bi]], [PSB[pa]])
                        for kc in range(8):
                            mm(PS[pf][:, :], wcs_s2[:, kc, n_ * 128:(n_ + 1) * 128], uvt[bi][:, kc, :], kc == 0, kc == 7,
                               [BwE, Buv[bi]], [PSB[pf]])
                        k2 = n_ % 2
                        S.op("dve", lambda e, pa=pa, k2=k2, n_=n_: e.tensor_tensor(
                            out=tA[k2][:], in0=PS[pa][:, :], in1=gblk[bi][:, n_, :], op=ALU.mult),
                            reads=[PSB[pa], Bg[bi]], writes=[BtA[k2]])
                        S.op("dve", lambda e, pf=pf, k2=k2, n_=n_: e.tensor_tensor(
                            out=tB[k2][:], in0=PS[pf][:, :], in1=gblk[bi][:, 8 + n_, :], op=ALU.mult),
                            reads=[PSB[pf], Bg[bi]], writes=[BtB[k2]])
                        S.op("pool", lambda e, k2=k2, n_=n_: e.tensor_tensor(
                            out=mT[bi][:, n_, :], in0=tA[k2][:], in1=tB[k2][:], op=ALU.add),
                            reads=[BtA[k2], BtB[k2]], partial=[BmT[bi]])
                        yield

                def O_tile(g, i):
                    b, tb = divmod(g, 4)
                    ib, bi = b % 2, g % 2
                    ti = tb * 4 + i
                    col = b * NT + ti
                    k = (g * 4 + i) % 2
                    r0 = ti * 128
                    S.dma("sp", lambda e: e.dma_start(out=xt[k][:], in_=x4[b, r0:r0 + 128, :]), writes=[BxtE[k]])
                    for nh in range(2):
                        pw = 2 + nh
                        for kc in range(8):
                            mm(PS[pw][:, :], mT[bi][:, kc, i * 128:(i + 1) * 128], wout_s[:, kc, nh * 512:(nh + 1) * 512],
                               kc == 0, kc == 7, [BmT[bi], BwE], [PSB[pw]])
                        S.op("dve", lambda e, pw=pw, nh=nh: e.tensor_tensor(
                            out=tt[:, nh * 512:(nh + 1) * 512], in0=PS[pw][:, :], in1=g1bc[ib][:, nh * 512:(nh + 1) * 512], op=ALU.mult),
                            reads=[PSB[pw], Bg1[ib]], partial=[Btt])
                    S.op("pool", lambda e: e.tensor_tensor(out=x1[k][:], in0=tt[:], in1=xt[k][:], op=ALU.add),
                         reads=[Btt, BxtE[k]], writes=[Bx1[k]])
                    S.dma("pool", lambda e: e.dma_start(out=out[b, r0:r0 + 128, :], in_=x1[k][:]), reads=[Bx1[k]])
                    S.op("act", lambda e: e.activation(out=junk[:], in_=x1[k][:], func=AF.Square, accum_out=ss2[:, col:col + 1]),
                         reads=[Bx1[k], Bss2], writes=[BjE], partial=[Bss2])
                    S.op("act", lambda e: e.activation(
                        out=ln2[:, col:col + 1], in_=ss2[:, col:col + 1], func=AF.Ln, scale=1.0 / D, bias=epsb[:, 0:1]),
                        reads=[Bss2, Bconst], partial=[Bln2])
                    S.op("act", lambda e: e.activation(
                        out=rs2[:, col:col + 1], in_=ln2[:, col:col + 1], func=AF.Exp, scale=-0.5), reads=[Bln2], partial=[Brs2])
                    S.op("dve", lambda e: e.scalar_tensor_tensor(
                        out=tt2[:], in0=x1[k][:], scalar=rs2[:, col:col + 1], in1=a2bc[ib][:], op0=ALU.mult, op1=ALU.mult),
                        reads=[Bx1[k], Brs2, Ba2[ib]], writes=[Btt2])
                    S.op("pool", lambda e: e.tensor_tensor(out=h2s[k][:], in0=tt2[:], in1=s2bc[ib][:], op=ALU.add),
                         reads=[Btt2, Bs2[ib]], writes=[Bh2s[k]])
                    S.dma("pool", lambda e: e.dma_start(out=h2_d[b, r0:r0 + 128, :], in_=h2s[k][:]), reads=[Bh2s[k]])

                def R_tile(g, i):
                    b, tb = divmod(g, 4)
                    ti = tb * 4 + i
                    k = (g * 4 + i) % 2
                    pt = 6 + (cE["t"] % 2)
                    cE["t"] += 1
                    ptv = PS[pt][:].bitcast(BF16).rearrange("p (k t) -> p k t", k=8)
                    for kc in range(8):
                        tr(ptv[:, kc, :], h2s[k][:, kc * 128:(kc + 1) * 128], ident_b[:], [Bh2s[k], Bconst],
                           writes=[PSB[pt]] if kc == 0 else (), partial=() if kc == 0 else [PSB[pt]])
                    S.op("act", lambda e: e.activation(out=h2T[k][:], in_=ptv, func=AF.Copy), reads=[PSB[pt]], writes=[Bh2T[k]])
                    yield
                    for kc in range(8):
                        mm(PS[5][:, 0:NE], h2T[k][:, kc, :], wr_s[:, kc, :], kc == 0, kc == 7, [Bh2T[k], BwE], [PSB[5]])
                    S.op("dve", lambda e: e.tensor_reduce(out=mx[:, 0:1], in_=PS[5][:, 0:NE], axis=AX.X, op=ALU.max),
                         reads=[PSB[5]], writes=[Bmx])
                    S.op("dve", lambda e: e.tensor_scalar(out=mx[:, 1:2], in0=mx[:, 0:1], scalar1=-1.0, scalar2=None, op0=ALU.mult),
                         reads=[Bmx], writes=[Bmx])
                    S.op("act", lambda e: e.activation(out=ex[:], in_=PS[5][:, 0:NE], func=AF.Exp, bias=mx[:, 1:2]),
                         reads=[PSB[5], Bmx], writes=[Bex])
                    S.op("dve", lambda e: e.tensor_reduce(out=sm[:, 0:1], in_=ex[:], axis=AX.X, op=ALU.add),
                         reads=[Bex], writes=[Bsm])
                    S.op("dve", lambda e: e.reciprocal(out=sm[:, 1:2], in_=sm[:, 0:1]), reads=[Bsm], writes=[Bsm])
                    S.op("dve", lambda e: e.tensor_scalar(
                        out=aff_all[:, ti, b, :], in0=ex[:], scalar1=sm[:, 1:2], scalar2=None, op0=ALU.mult),
                        reads=[Bex, Bsm], partial=[Baff])

                def drain(gen):
                    for _ in gen:
                        pass

                NG = NB * 4
                load_dft(0, 0)
                load_dft(0, 1)
                drain(fourier(0))
                drain(merge(0))
                for g in range(1, NG + 1):
                    fgen = fourier(g) if g < NG else iter(())
                    for kind, i in (("O", 0), ("O", 1), ("R", 0), ("O", 2), ("R", 1), ("O", 3)):
                        next(fgen, None)
                        if kind == "O":
                            O_tile(g - 1, i)
                        else:
                            rg = R_tile(g - 1, i)
                            next(rg, None)
                            next(fgen, None)
                            drain(rg)
                    drain(fgen)
                    mgen = merge(g) if g < NG else iter(())
                    for i in (2, 3):
                        next(mgen, None)
                        next(mgen, None)
                        rg = R_tile(g - 1, i)
                        next(rg, None)
                        next(mgen, None)
                        next(mgen, None)
                        drain(rg)
                    drain(mgen)
            S.barrier()

        if run("E"):
            phase_E()

        def phase_F():
            with ExitStack() as st:
                def sb(name, shape, dt):
                    return st.enter_context(nc.sbuf_tensor(name, list(shape), dt))

                NBE = NB * NE
                iota_f = sb("iota_f", [128, 256], F32)
                iota = sb("iota", [128, 256], BF16)
                tri = sb("tri_s", [128, 128], BF16)
                rvc_s = sb("rvc_s", [128, NT, NE, 2], BF16)
                rv = sb("rv", [128, NT, NB, NE, 4], BF16)
                affT = sb("affT", [NBE, T], F32)
                wk = [sb(f"wk{i}", [NBE, T], F32) for i in range(2)]
                m8 = sb("m8", [NBE, 8], F32)
                maskT = sb("maskT", [NBE, T], F32)
                mask_f = sb("mask_f", [128, NT, NBE], F32)
                mask_b = sb("mask_b", [128, NT, NBE], BF16)
                posm = sb("posm", [128, NT, NBE], F32)
                oh = [sb(f"oh{i}", [128, 256], BF16) for i in range(4)]
                hi_f = sb("hi_f", [128, NT, NB, NE], F32)
                res = [sb(f"res{i}", [128, 2, 4], F32) for i in range(2)]
                idf = [sb(f"idf{i}", [128, 2], F32) for i in range(2)]
                BcF, Brv, BaffT, Bm8, BmaskT, Bmf, Bmb, Bposm, Bhif = (Buf() for _ in range(9))
                Bwk, Boh, Bres, Bidf = bufs(2), bufs(4), bufs(2), bufs(2)
                S.dma("sp", lambda e: e.dma_start(out=iota_f[:], in_=iota_d), partial=[BcF])
                S.dma("sp", lambda e: e.dma_start(out=tri[:], in_=tri_d), partial=[BcF])
                S.dma("sp", lambda e: e.dma_start(out=rvc_s[:], in_=rvc_d), partial=[BcF])
                S.op("dve", lambda e: e.tensor_copy(out=iota[:], in_=iota_f[:]), reads=[BcF], partial=[BcF])
                aff_v = aff_all[:]
                for b in range(NB):
                    S.op("dve", lambda e, b=b: e.tensor_copy(out=rv[:, :, b, :, 0:2], in_=rvc_s[:]), reads=[BcF], partial=[Brv])
                S.op("dve", lambda e: e.tensor_copy(out=rv[:, :, :, :, 2], in_=aff_v), reads=[Baff], partial=[Brv])
                S.op("dve", lambda e: e.tensor_copy(out=hi_f[:], in_=rv[:, :, :, :, 2]), reads=[Brv], writes=[Bhif])
                S.op("dve", lambda e: e.tensor_tensor(out=hi_f[:], in0=aff_v, in1=hi_f[:], op=ALU.subtract),
                     reads=[Baff, Bhif], writes=[Bhif])
                S.op("dve", lambda e: e.tensor_copy(out=rv[:, :, :, :, 3], in_=hi_f[:]), reads=[Bhif], partial=[Brv])
                for g in range(4):
                    pb = g % 2
                    for j in range(4):
                        i = g * 4 + j
                        tr(PS[pb][0:NBE, j * 128:(j + 1) * 128], aff_all[:, i, :, :].rearrange("p b e -> p (b e)"), ident_f[:], [Baff, Bconst],
                           writes=[PSB[pb]] if j == 0 else (), partial=() if j == 0 else [PSB[pb]])
                    S.op("act", lambda e, pb=pb, g=g: e.activation(out=affT[:, g * 512:(g + 1) * 512], in_=PS[pb][0:NBE, :], func=AF.Copy),
                         reads=[PSB[pb]], partial=[BaffT])
                cur, curB = affT, BaffT
                for it in range(CAP // 8):
                    S.op("dve", lambda e, cur=cur: e.max(out=m8[:], in_=cur[:]), reads=[curB], writes=[Bm8])
                    if it < CAP // 8 - 1:
                        nxt, nB = wk[it % 2], Bwk[it % 2]
                        S.op("dve", lambda e, cur=cur, nxt=nxt: e.match_replace(
                            out=nxt[:], in_to_replace=m8[:], in_values=cur[:], imm_value=-1.0), reads=[curB, Bm8], writes=[nB])
                        cur, curB = nxt, nB
                S.op("dve", lambda e: e.tensor_scalar(out=maskT[:], in0=affT[:], scalar1=m8[:, 7:8], scalar2=None, op0=ALU.is_ge),
                     reads=[BaffT, Bm8], writes=[BmaskT])
                for g in range(2):
                    pb = 2 + g
                    for j in range(8):
                        i = g * 8 + j
                        tr(PS[pb][:, j * NBE:(j + 1) * NBE], maskT[:, i * 128:(i + 1) * 128], ident_f[0:NBE, 0:NBE], [BmaskT, Bconst],
                           writes=[PSB[pb]] if j == 0 else (), partial=() if j == 0 else [PSB[pb]])
                    S.op("act", lambda e, pb=pb, g=g: e.activation(
                        out=mask_f[:, g * 8:(g + 1) * 8, :], in_=PS[pb][:, :].rearrange("p (j n) -> p j n", j=8), func=AF.Copy),
                        reads=[PSB[pb]], partial=[Bmf])
                S.op("dve", lambda e: e.tensor_copy(out=mask_b[:], in_=mask_f[:]), reads=[Bmf], writes=[Bmb])
                for i in range(NT):
                    pb = 4 + (i % 2)
                    for j in range(i + 1):
                        mm(PS[pb][:, 0:NBE], ones_b[:] if j < i else tri[:], mask_b[:, j, :], j == 0, j == i,
                           [Bmb, Bconst, BcF], [PSB[pb]])
                    S.op("dve", lambda e, pb=pb, i=i: e.scalar_tensor_tensor(
                        out=posm[:, i, :], in0=PS[pb][:, 0:NBE], scalar=1.0, in1=mask_f[:, i, :], op0=ALU.add, op1=ALU.mult),
                        reads=[PSB[pb], Bmf], partial=[Bposm])
                S.op("dve", lambda e: e.tensor_scalar_add(out=posm[:], in0=posm[:], scalar1=-1.0), reads=[Bposm], writes=[Bposm])
                ohc = 0
                for b in range(NB):
                    for ex_ in range(NE):
                        col = b * NE + ex_
                        r_i = col % 2
                        pbase = 2 * r_i
                        for i in range(NT):
                            io = ohc % 4
                            ohc += 1
                            S.op("dve", lambda e, io=io, i=i, col=col: e.tensor_scalar(
                                out=oh[io][:], in0=iota[:], scalar1=posm[:, i, col:col + 1], scalar2=None, op0=ALU.is_equal),
                                reads=[BcF, Bposm], writes=[Boh[io]])
                            for half in range(2):
                                mm(PS[pbase + half][:, 0:4], oh[io][:, half * 128:(half + 1) * 128], rv[:, i, b, ex_, :], i == 0, i == NT - 1,
                                   [Boh[io], Brv], [PSB[pbase + half]])
                        for half in range(2):
                            S.op("act", lambda e, half=half, r_i=r_i, pbase=pbase: e.activation(
                                out=res[r_i][:, half, :], in_=PS[pbase + half][:, 0:4], func=AF.Copy),
                                reads=[PSB[pbase + half]], partial=[Bres[r_i]])
                        S.op("dve", lambda e, r_i=r_i, b=b: e.tensor_scalar(
                            out=idf[r_i][:], in0=res[r_i][:, :, 0], scalar1=128.0, scalar2=float(b * T), op0=ALU.mult, op1=ALU.add),
                            reads=[Bres[r_i]], writes=[Bidf[r_i]])
                        S.op("dve", lambda e, r_i=r_i: e.tensor_tensor(out=idf[r_i][:], in0=idf[r_i][:], in1=res[r_i][:, :, 1], op=ALU.add),
                             reads=[Bres[r_i], Bidf[r_i]], writes=[Bidf[r_i]])
                        S.op("dve", lambda e, r_i=r_i, b=b, ex_=ex_: e.tensor_copy(out=idx_all[:, b, ex_, :], in_=idf[r_i][:]),
                             reads=[Bidf[r_i]], partial=[Bidx])
                        S.op("dve", lambda e, r_i=r_i, b=b, ex_=ex_: e.tensor_tensor(
                            out=gate_all[:, b, ex_, :], in0=res[r_i][:, :, 2], in1=res[r_i][:, :, 3], op=ALU.add),
                            reads=[Bres[r_i]], partial=[Bidx])
            S.barrier()

        if run("F"):
            phase_F()
        elif "F" in skip:
            S.op("pool", lambda e: e.iota(idx_all[:], pattern=[[0, NB * NE * 2]], base=0, channel_multiplier=1), writes=[Bidx])
            S.op("dve", lambda e: e.memset(gate_all[:], 0.0), partial=[Bidx])

        def phase_G():
            with ExitStack() as st:
                def sb(name, shape, dt):
                    return st.enter_context(nc.sbuf_tensor(name, list(shape), dt))

                stg = [sb(f"stgG{i}", [128, 4096], F32) for i in range(3)]
                wg = [sb(f"wg{i}", [128, 8, 512], BF16) for i in range(2)]
                wu = [sb(f"wu{i}", [128, 8, 512], BF16) for i in range(2)]
                wd = [sb(f"wd{i}", [128, 4, D], BF16) for i in range(2)]
                g2bc = [sb(f"g2bc{i}", [128, D], F32) for i in range(NB)]
                xe = [sb(f"xe{i}", [128, D], BF16) for i in range(6)]
                xeT = [sb(f"xeT{i}", [128, 8, 256], BF16) for i in range(2)]
                sg = [sb(f"sg{i}", [128, 256], F32) for i in range(2)]
                actT = [sb(f"actT{i}", [128, 4, 256], BF16) for i in range(2)]
                yo = [sb(f"yo{i}", [128, D], F32) for i in range(3)]
                Bstg, Bwg, Bwu, Bwd = bufs(3), bufs(2), bufs(2), bufs(2)
                Bg2, Bxe, BxeT, Bsg, BactT, Byo = bufs(NB), bufs(6), bufs(2), bufs(2), bufs(2), bufs(3)
                Bout = bufs(NB)
                for b in range(NB):
                    S.dma("sp", lambda e, b=b: e.dma_start(out=g2bc[b][:], in_=modr[b, 5 * D:6 * D].partition_broadcast(128)),
                          writes=[Bg2[b]])
                steps = [(ex_, b) for ex_ in range(NE) for b in range(NB)]
                NS = len(steps)
                wsrc = lambda ex_: ((w_e_gate[ex_], wg[ex_ % 2], Bwg[ex_ % 2], 8), (w_e_up[ex_], wu[ex_ % 2], Bwu[ex_ % 2], 8),
                                    (w_e_down[ex_], wd[ex_ % 2], Bwd[ex_ % 2], 4))
                cc_ = {"y": 0, "p": 0, "t": 0}

                def w_load(ex_):
                    if ex_ >= NE:
                        return
                    for j, (src, dst, dB, a) in enumerate(wsrc(ex_)):
                        sv = stg[j][:].rearrange("p (a n) -> p a n", a=a)
                        S.dma("sp", lambda e, sv=sv, src=src: e.dma_start(out=sv, in_=src.rearrange("(a p) n -> p a n", p=128)),
                              writes=[Bstg[j]])

                def w_cast(ex_, j):
                    if ex_ >= NE:
                        return
                    src, dst, dB, a = wsrc(ex_)[j]
                    sv = stg[j][:].rearrange("p (a n) -> p a n", a=a)
                    if j < 2:
                        S.op("act", lambda e: e.activation(out=dst[:], in_=sv, func=AF.Copy), reads=[Bstg[j]], writes=[dB])
                    else:
                        S.op("dve", lambda e: e.tensor_copy(out=dst[:], in_=sv), reads=[Bstg[j]], writes=[dB])

                def S1(s_):
                    if s_ >= NS:
                        return
                    ex_, b = steps[s_]
                    for half in range(2):
                        kx = (2 * s_ + half) % 6
                        S.dma("pool", lambda e, kx=kx, half=half: e.indirect_dma_start(
                            out=xe[kx][:], out_offset=None, in_=h2_d.rearrange("b t d -> (b t) d"),
                            in_offset=bass.IndirectOffsetOnAxis(ap=idx_all[:, b, ex_, half:half + 1], axis=0)),
                            reads=[Bidx], writes=[Bxe[kx]])

                def S2(s_, half):
                    if s_ >= NS:
                        return
                    kx = (2 * s_ + half) % 6
                    ix = s_ % 2
                    pt = 6 + (cc_["t"] % 2)
                    cc_["t"] += 1
                    ptv = PS[pt][:].bitcast(BF16).rearrange("p (k t) -> p k t", k=8)
                    for kc in range(8):
                        tr(ptv[:, kc, :], xe[kx][:, kc * 128:(kc + 1) * 128], ident_b[:], [Bxe[kx], Bconst],
                           writes=[PSB[pt]] if kc == 0 else (), partial=() if kc == 0 else [PSB[pt]])
                    S.op("act", lambda e: e.activation(out=xeT[ix][:, :, half * 128:(half + 1) * 128], in_=ptv, func=AF.Copy),
                         reads=[PSB[pt]], partial=[BxeT[ix]])

                def S3(s_, fc):
                    ex_, b = steps[s_]
                    ie, ix = ex_ % 2, s_ % 2
                    pg, pu = (0, 1) if fc % 2 == 0 else (2, 3)
                    for kc in range(8):
                        mm(PS[pg][:, 0:256], wg[ie][:, kc, fc * 128:(fc + 1) * 128], xeT[ix][:, kc, :], kc == 0, kc == 7,
                           [Bwg[ie], BxeT[ix]], [PSB[pg]])
                    for kc in range(8):
                        mm(PS[pu][:, 0:256], wu[ie][:, kc, fc * 128:(fc + 1) * 128], xeT[ix][:, kc, :], kc == 0, kc == 7,
                           [Bwu[ie], BxeT[ix]], [PSB[pu]])
                    ks = fc % 2
                    S.op("act", lambda e: e.activation(out=sg[ks][:], in_=PS[pg][:, 0:256], func=AF.Silu),
                         reads=[PSB[pg]], writes=[Bsg[ks]])
                    S.op("dve", lambda e: e.tensor_tensor(out=actT[ix][:, fc, :], in0=PS[pu][:, 0:256], in1=sg[ks][:], op=ALU.mult),
                         reads=[PSB[pu], Bsg[ks]], partial=[BactT[ix]])

                def S4(s_, half):
                    if s_ < 0:
                        return
                    ex_, b = steps[s_]
                    ie, ix = ex_ % 2, s_ % 2
                    ky = cc_["y"] % 3
                    cc_["y"] += 1
                    for nh in range(2):
                        py = 4 + (cc_["p"] % 2)
                        cc_["p"] += 1
                        for fc in range(4):
                            mm(PS[py][:, :], actT[ix][:, fc, half * 128:(half + 1) * 128], wd[ie][:, fc, nh * 512:(nh + 1) * 512],
                               fc == 0, fc == 3, [BactT[ix], Bwd[ie]], [PSB[py]])
                        S.op("dve", lambda e, py=py, nh=nh: e.scalar_tensor_tensor(
                            out=yo[ky][:, nh * 512:(nh + 1) * 512], in0=PS[py][:, :], scalar=gate_all[:, b, ex_, half:half + 1],
                            in1=g2bc[b][:, nh * 512:(nh + 1) * 512], op0=ALU.mult, op1=ALU.mult),
                            reads=[PSB[py], Bidx, Bg2[b]], partial=[Byo[ky]])
                    S.dma("pool", lambda e: e.indirect_dma_start(
                        out=out.rearrange("b t d -> (b t) d"), out_offset=bass.IndirectOffsetOnAxis(ap=idx_all[:, b, ex_, half:half + 1], axis=0),
                        in_=yo[ky][:], in_offset=None, compute_op=ALU.add),
                        reads=[Byo[ky], Bidx], writes=[Bout[b]])

                w_load(0)
                for j in range(3):
                    w_cast(0, j)
                S1(0)
                S1(1)
                S2(0, 0)
                S2(0, 1)
                for s_ in range(NS + 1):
                    if s_ < NS:
                        ex_, b = steps[s_]
                        if b == 0:
                            w_load(ex_ + 1)
                        else:
                            w_cast(ex_ + 1, b - 1)
                    S1(s_ + 2)
                    if s_ < NS:
                        S3(s_, 0)
                    S2(s_ + 1, 0)
                    if s_ < NS:
                        S3(s_, 1)
                    S4(s_ - 1, 0)
                    if s_ < NS:
                        S3(s_, 2)
                    S2(s_ + 1, 1)
                    if s_ < NS:
                        S3(s_, 3)
                    S4(s_ - 1, 1)
            S.barrier()
        if run("G"):
            phase_G()

        with nc.allow_low_precision("fp32 math, bf16 storage of matmul operands"):
            S.finish()
        print("instructions:", S.ninst)
    return nc


def _consts():
    bf = ml_dtypes.bfloat16
    t = np.arange(T)
    inv = (1.0 / (np.float32(10000.0) ** (np.arange(8, dtype=np.float32) / np.float32(8)))).astype(np.float32)
    pr = (t // 64).astype(np.float32)
    pcl = (t % 64).astype(np.float32)

    def tab(pos):
        ang = pos[:, None] * inv[None, :]
        ang = np.concatenate([ang, ang], axis=-1).astype(np.float32)
        return np.cos(ang).astype(np.float32), np.sin(ang).astype(np.float32)

    cr, sr = tab(pr)
    cc_, scc = tab(pcl)
    cos_t = np.concatenate([cr, cc_], axis=-1)
    sgn = np.concatenate([-np.ones(8), np.ones(8)]).astype(np.float32)
    sin_t = np.concatenate([sr * sgn, scc * sgn], axis=-1)
    to_tiles = lambda a: np.ascontiguousarray(a.reshape(NT, 128, 32).transpose(1, 0, 2))
    prod = (np.outer(t, t) % T).astype(np.float64) * (2 * np.pi / T)
    dft_c = np.cos(prod).astype(bf)
    dft_s = np.sin(prod).astype(bf)
    f = np.arange(128)
    pch = (np.outer(f, f) % 128).astype(np.float64) * (2 * np.pi / 128)
    rvc = np.zeros((128, NT, NE, 2), np.float32)
    rvc[:, :, :, 0] = np.arange(NT)[None, :, None]
    rvc[:, :, :, 1] = np.arange(128)[:, None, None]
    tri = (np.arange(128)[:, None] < np.arange(128)[None, :]).astype(np.float32)
    return dict(
        rope_cos=to_tiles(cos_t), rope_sin=to_tiles(sin_t), dft_c=dft_c, dft_s=dft_s,
        ch_c=np.cos(pch).astype(np.float32), ch_s=np.sin(pch).astype(np.float32),
        ident_b=np.eye(128, dtype=np.float32).astype(bf), ident_f=np.eye(128, dtype=np.float32),
        iota256=np.ascontiguousarray(np.broadcast_to(np.arange(256, dtype=np.float32)[None, :], (128, 256))),
        tri=tri.astype(bf), rvc=rvc.astype(bf),
    )


def make_in_maps(x, c, ctx, c_ctx, w_mod, b_mod, norm1_g, w_in, q_a_norm_g, kv_a_norm_g, w_q_up, w_kv_up, q_norm_g,
                 k_norm_g, w_o_attn, w_fourier, w_out, norm2_g, w_router, w_e_gate, w_e_up, w_e_down):
    f32 = lambda a: np.ascontiguousarray(np.asarray(a, dtype=np.float32))
    consts = _consts()
    fm = lambda v, k: np.ascontiguousarray(f32(v).reshape(k, 128).T)
    bc = lambda v: np.ascontiguousarray(np.broadcast_to(f32(v)[None, :], (128, v.shape[-1])))
    shared = dict(
        bmod5=np.ascontiguousarray(np.broadcast_to(f32(b_mod[0])[None, :], (5, 6 * D))),
        n1g=fm(norm1_g[0], 8), n2g=fm(norm2_g[0], 8), n2g_bc=bc(norm2_g[0]),
        qag=fm(q_a_norm_g[0], 3), kvag=fm(kv_a_norm_g[0], 2), qg_bc=bc(q_norm_g[0]), kg_bc=bc(k_norm_g[0]),
        w_mod=f32(w_mod[0]), w_in=f32(w_in[0]), w_q_up=f32(w_q_up[0]), w_kv_up=f32(w_kv_up[0]),
        w_o_attn=f32(w_o_attn[0]), w_fourier=f32(w_fourier[0]), w_out=f32(w_out[0]), w_router=f32(w_router[0]),
        w_e_gate=f32(w_e_gate[0]), w_e_up=f32(w_e_up[0]), w_e_down=f32(w_e_down[0]), **consts)
    x = f32(x)
    ctx = f32(ctx)
    c = f32(c)
    c_ctx = f32(c_ctx)
    maps = []
    for core in range(8):
        sl = slice(core * NB, (core + 1) * NB)
        cc5 = np.concatenate([c[sl], c_ctx[None, :]], axis=0)
        cTl = np.ascontiguousarray(cc5.reshape(5, 8, 128).transpose(2, 1, 0).reshape(128, 40))
        maps.append(dict(x4=x[sl], ctx4=ctx[sl], cT=cTl, **shared))
    return maps


def kernel(**inputs):
    nc = build_program()
    maps = make_in_maps(**inputs)
    res = run_bass_kernel_spmd(nc, maps, core_ids=list(range(8)))
    return np.concatenate([np.asarray(r["out"], dtype=np.float32) for r in res.results], axis=0)
```

```python
import numpy as np
import ml_dtypes
from contextlib import ExitStack
import concourse.bass as bass
import concourse.mybir as mybir
from concourse.bass_utils import run_bass_kernel_spmd

F32 = mybir.dt.float32
BF16 = mybir.dt.bfloat16
I32 = mybir.dt.int32
ALU = mybir.AluOpType
AF = mybir.ActivationFunctionType
AX = mybir.AxisListType

NB = 4
T = 2048
D = 1024
CTX = 256
NT = T // 128
H = 8
QKD = 96
NIN = 3232
OFF_Q, OFF_KV, OFF_KPE, OFF_F, OFF_GA, OFF_GB = 0, 384, 640, 672, 1184, 2208
NE = 16
CAP = 256
EPS = 1e-6

SEM_LIMIT = 30000
SAME_ENGINE_SYNC = True


class Buf:
    __slots__ = ("w", "r", "name")

    def __init__(self, name=""):
        self.w = {}
        self.r = {}
        self.name = name


def bufs(n):
    return [Buf() for _ in range(n)]


class _Eng:
    def __init__(self, name, ndma):
        self.name = name
        self.sems = []
        self.count = 0
        self.waited = {}
        self.prog = []
        self.dma_sems = []
        self.dma_vals = []
        self.dma_next = 0
        self.ndma = ndma


class Sched:
    def __init__(self, nc, stack):
        self.nc = nc
        self.stack = stack
        ndma = {"sp": 32, "act": 8, "pool": 32, "pe": 0, "dve": 0}
        self.eng = {n: _Eng(n, ndma[n]) for n in ("pe", "act", "dve", "pool", "sp")}
        for e in self.eng.values():
            self._new_sem(e)
            for i in range(e.ndma):
                s = stack.enter_context(nc.semaphore(f"d_{e.name}{i}"))
                e.dma_sems.append(s)
                e.dma_vals.append(0)
        self.ninst = 0

    def _new_sem(self, e):
        s = self.stack.enter_context(self.nc.semaphore(f"e_{e.name}{len(e.sems)}"))
        e.sems.append(s)
        e.count = 0

    def _deps(self, e, reads, writes, partial):
        deps = {}

        def upd(d):
            for s, v in d.items():
                if deps.get(s, 0) < v:
                    deps[s] = v

        for b in reads:
            upd(b.w)
        for b in writes:
            upd(b.w)
            upd(b.r)
        for b in partial:
            upd(b.r)
        waits = []
        own = e.sems
        for s, v in deps.items():
            if (e.name == "pe" or not SAME_ENGINE_SYNC) and any(s is o for o in own):
                continue
            if e.waited.get(s, 0) < v:
                e.waited[s] = v
                waits.append((s, v))
        return waits

    def _mark(self, sem, val, reads, writes, partial):
        for b in reads:
            if b.r.get(sem, 0) < val:
                b.r[sem] = val
        for b in writes:
            b.w = {sem: val}
            b.r = {}
        for b in partial:
            if b.w.get(sem, 0) < val:
                b.w[sem] = val

    def op(self, en, fn, reads=(), writes=(), partial=()):
        e = self.eng[en]
        waits = self._deps(e, reads, writes, partial)
        if e.count >= SEM_LIMIT:
            self._new_sem(e)
        e.count += 1
        sem, val = e.sems[-1], e.count

        def emit(eo, waits=waits, fn=fn, sem=sem):
            for s, v in waits:
                eo.wait_ge(s, v)
            fn(eo).then_inc(sem, 1)

        e.prog.append(emit)
        self.ninst += 1
        self._mark(sem, val, reads, writes, partial)

    def dma(self, en, fn, reads=(), writes=(), partial=()):
        e = self.eng[en]
        waits = self._deps(e, reads, writes, partial)
        k = e.dma_next
        e.dma_next = (k + 1) % e.ndma
        sem = e.dma_sems[k]
        prev = e.dma_vals[k]
        if prev > 0 and e.waited.get(sem, 0) < prev:
            e.waited[sem] = prev
            waits.append((sem, prev))
        val = prev + 16
        e.dma_vals[k] = val

        def emit(eo, waits=waits, fn=fn, sem=sem):
            for s, v in waits:
                eo.wait_ge(s, v)
            fn(eo).then_inc(sem, 16)

        e.prog.append(emit)
        self.ninst += 1
        self._mark(sem, val, reads, writes, partial)

    def _all_ticks(self):
        ticks = []
        for e in self.eng.values():
            for i, s in enumerate(e.sems):
                v = e.count if i == len(e.sems) - 1 else SEM_LIMIT
                if v > 0:
                    ticks.append((s, v))
            for s, v in zip(e.dma_sems, e.dma_vals):
                if v > 0:
                    ticks.append((s, v))
        return ticks

    def barrier(self):
        ticks = self._all_ticks()
        for e in self.eng.values():
            ws = []
            for s, v in ticks:
                if e.waited.get(s, 0) < v:
                    e.waited[s] = v
                    ws.append((s, v))

            def emit(eo, ws=ws):
                for s, v in ws:
                    eo.wait_ge(s, v)

            e.prog.append(emit)

    def finish(self):
        self.barrier()
        nc = self.nc
        with nc.Block() as block:
            @block.sync
            def _(eo):
                for f in self.eng["sp"].prog:
                    f(eo)

            @block.tensor
            def _(eo):
                for f in self.eng["pe"].prog:
                    f(eo)

            @block.scalar
            def _(eo):
                for f in self.eng["act"].prog:
                    f(eo)

            @block.vector
            def _(eo):
                for f in self.eng["dve"].prog:
                    f(eo)

            @block.gpsimd
            def _(eo):
                for f in self.eng["pool"].prog:
                    f(eo)


def build_program(stop_after="G", dbg=(), skip=()):
    nc = bass.Bass("TRN2", target_bir_lowering=False)

    def din(name, shape, dt):
        return nc.dram_tensor(name, list(shape), dt, kind="ExternalInput").ap()

    def dscr(name, shape, dt):
        kind = "ExternalOutput" if name in dbg else "Internal"
        return nc.dram_tensor(name, list(shape), dt, kind=kind).ap()

    x4 = din("x4", [NB, T, D], F32)
    ctx4 = din("ctx4", [NB, CTX, D], F32)
    cT = din("cT", [128, 40], F32)
    bmod5 = din("bmod5", [5, 6 * D], F32)
    n1g = din("n1g", [128, 8], F32)
    n2g = din("n2g", [128, 8], F32)
    n2g_bc = din("n2g_bc", [128, D], F32)
    qag = din("qag", [128, 3], F32)
    kvag = din("kvag", [128, 2], F32)
    qg_bc = din("qg_bc", [128, QKD], F32)
    kg_bc = din("kg_bc", [128, QKD], F32)
    w_mod = din("w_mod", [D, 6 * D], F32)
    w_in = din("w_in", [D, NIN], F32)
    w_q_up = din("w_q_up", [384, 768], F32)
    w_kv_up = din("w_kv_up", [256, 1024], F32)
    w_o_attn = din("w_o_attn", [512, D], F32)
    w_fourier = din("w_fourier", [512, D], F32)
    w_out = din("w_out", [D, D], F32)
    w_router = din("w_router", [D, NE], F32)
    w_e_gate = din("w_e_gate", [NE, D, 512], F32)
    w_e_up = din("w_e_up", [NE, D, 512], F32)
    w_e_down = din("w_e_down", [NE, 512, D], F32)
    rope_cos = din("rope_cos", [128, NT, 32], F32)
    rope_sin = din("rope_sin", [128, NT, 32], F32)
    dft_c = din("dft_c", [T, T], BF16)
    dft_s = din("dft_s", [T, T], BF16)
    ch_c = din("ch_c", [128, 128], F32)
    ch_s = din("ch_s", [128, 128], F32)
    ident_b_d = din("ident_b", [128, 128], BF16)
    ident_f_d = din("ident_f", [128, 128], F32)
    iota_d = din("iota256", [128, 256], F32)
    tri_d = din("tri", [128, 128], BF16)
    rvc_d = din("rvc", [128, NT, NE, 2], BF16)
    out = nc.dram_tensor("out", [NB, T, D], F32, kind="ExternalOutput").ap()

    modr = dscr("modr", [5, 6 * D], F32)
    w_in_bf = dscr("w_in_bf", [D, NIN], BF16)
    wq_bf = dscr("wq_bf", [384, 768], BF16)
    wkv_bf = dscr("wkv_bf", [256, 1024], BF16)
    wo_bf = dscr("wo_bf", [512, D], BF16)
    wcs_bf = dscr("wcs_bf", [1024, D], BF16)
    wout_bf = dscr("wout_bf", [D, D], BF16)
    pqkvT = dscr("pqkvT", [NB, 5, 128, T], BF16)
    pkvcT = dscr("pkvcT", [NB, 2, 128, CTX], BF16)
    kpe_d = dscr("kpe_d", [NB, T + CTX, 32], F32)
    Ftm = dscr("Ftm", [NB, T, 512], BF16)
    gT = dscr("gT", [NB, 16, 128, T], BF16)
    qT_d = dscr("qT_d", [NB, H, QKD, T], BF16)
    kT_d = dscr("kT_d", [NB, H, QKD, T + CTX], BF16)
    vx_d = dscr("vx_d", [NB, T + CTX, H * 65], BF16)
    attnT = dscr("attnT", [NB, 4, 128, T], BF16)
    h2_d = dscr("h2_d", [NB, T, D], BF16)
    dbg_d = dscr("dbg_d", [128, 4096], F32)

    order = ["0", "A", "B", "C", "E", "F", "G"]
    stop_i = order.index(stop_after)

    def run(p):
        return order.index(p) <= stop_i and p not in skip

    with ExitStack() as gst:
        S = Sched(nc, gst)

        def sbg(name, shape, dt):
            return gst.enter_context(nc.sbuf_tensor(name, list(shape), dt))

        PSBIG = gst.enter_context(nc.psum_tensor("psbig", [128, 4096], F32))
        PS = [PSBIG[:, i * 512:(i + 1) * 512] for i in range(8)]
        PSB = bufs(8)

        ident_b = sbg("ident_b_s", [128, 128], BF16)
        ident_f = sbg("ident_f_s", [128, 128], F32)
        tbl = sbg("tbl", [128, 4, 5, 8], F32)
        a1 = sbg("a1", [128, 5, 8], F32)
        a2 = sbg("a2", [128, 5, 8], F32)
        n1g_s = sbg("n1g_s", [128, 8], F32)
        n2g_s = sbg("n2g_s", [128, 8], F32)
        epsb = sbg("epsb", [128, 1], F32)
        ones_b = sbg("ones_b", [128, 128], BF16)
        rstd_qkv = sbg("rstd_qkv", [128, NB, 2, NT + 2], F32)
        aff_all = sbg("aff_all", [128, NT, NB, NE], F32)
        idx_all = sbg("idx_all", [128, NB, NE, 2], I32)
        gate_all = sbg("gate_all", [128, NB, NE, 2], F32)
        Bconst, Btbl, Brstd, Baff, Bidx = Buf(), Buf(), Buf(), Buf(), Buf()

        def mm(out_ap, lhsT, rhs, start, stop, reads, writes, partial=()):
            S.op("pe", lambda e: e.matmul(out_ap, lhsT=lhsT, rhs=rhs, start=start, stop=stop),
                 reads=reads, writes=writes, partial=partial)

        def tr(out_ap, in_ap, ident, reads, writes=(), partial=()):
            S.op("pe", lambda e: e.transpose(out=out_ap, in_=in_ap, identity=ident),
                 reads=reads, writes=writes, partial=partial)

        def rstd_from_ss(dst, src, scale, reads, writes, tmp, tmpb):
            S.op("act", lambda e: e.activation(out=tmp, in_=src, func=AF.Ln, scale=scale, bias=epsb[:, 0:1]),
                 reads=list(reads) + [Bconst], writes=[tmpb])
            S.op("act", lambda e: e.activation(out=dst, in_=tmp, func=AF.Exp, scale=-0.5),
                 reads=[tmpb], partial=writes)

        with ExitStack() as st:
            def sb(name, shape, dt):
                return st.enter_context(nc.sbuf_tensor(name, list(shape), dt))

            S.dma("sp", lambda e: e.dma_start(out=ident_b[:], in_=ident_b_d), writes=[Bconst])
            S.dma("sp", lambda e: e.dma_start(out=ident_f[:], in_=ident_f_d), partial=[Bconst])
            S.dma("sp", lambda e: e.dma_start(out=n1g_s[:], in_=n1g), partial=[Bconst])
            S.dma("sp", lambda e: e.dma_start(out=n2g_s[:], in_=n2g), partial=[Bconst])
            S.op("dve", lambda e: e.memset(epsb[:], EPS), partial=[Bconst])
            S.op("dve", lambda e: e.memset(ones_b[:], 1.0), partial=[Bconst])

            cts = sb("cts", [128, 40], F32)
            sT = sb("sT", [128, 40], F32)
            et = sb("et", [128, 40], F32)
            Bc, BsT, Bet = Buf(), Buf(), Buf()
            S.dma("sp", lambda e: e.dma_start(out=cts[:], in_=cT), writes=[Bc])
            S.op("act", lambda e: e.activation(out=et[:], in_=cts[:], func=AF.Exp, scale=-1.0), reads=[Bc], writes=[Bet])
            S.op("dve", lambda e: e.tensor_scalar_add(out=et[:], in0=et[:], scalar1=1.0), reads=[Bet], writes=[Bet])
            S.op("dve", lambda e: e.reciprocal(out=et[:], in_=et[:]), reads=[Bet], writes=[Bet])
            S.op("dve", lambda e: e.tensor_mul(out=sT[:], in0=cts[:], in1=et[:]), reads=[Bc, Bet], writes=[BsT])

            stg0 = [sb(f"stg0{i}", [128, 4096], F32) for i in range(3)]
            Bstg0 = bufs(3)
            modrows = sb("modrows", [5, 6 * D], F32)
            bm = sb("bm", [5, 6 * D], F32)
            Bmr, Bbm = Buf(), Buf()
            S.dma("sp", lambda e: e.dma_start(out=bm[:], in_=bmod5), writes=[Bbm])
            for nb in range(12):
                k = nb % 3
                wv = stg0[k][:].rearrange("p (k n) -> p k n", k=8)
                S.dma("sp", lambda e, wv=wv, nb=nb: e.dma_start(
                    out=wv, in_=w_mod[:, nb * 512:(nb + 1) * 512].rearrange("(k p) n -> p k n", p=128)),
                    writes=[Bstg0[k]])
                pb = nb % 2
                for kc in range(8):
                    mm(PS[pb][0:5, :], sT[:, kc * 5:(kc + 1) * 5], wv[:, kc, :], kc == 0, kc == 7,
                       [BsT, Bstg0[k]], [PSB[pb]])
                S.op("dve", lambda e, nb=nb, pb=pb: e.tensor_tensor(
                    out=modrows[:, nb * 512:(nb + 1) * 512], in0=PS[pb][0:5, :], in1=bm[:, nb * 512:(nb + 1) * 512],
                    op=ALU.add), reads=[PSB[pb], Bbm], partial=[Bmr])
            Bmodr = Buf()
            S.dma("sp", lambda e: e.dma_start(out=modr, in_=modrows[:]), reads=[Bmr], writes=[Bmodr])
            for gi, grp in enumerate((0, 1, 3, 4)):
                for b in range(5):
                    S.dma("sp", lambda e, gi=gi, grp=grp, b=b: e.dma_start(
                        out=tbl[:, gi, b, :], in_=modr[b, grp * D:(grp + 1) * D].rearrange("(k p) -> p k", p=128),
                        allow_slow_non_contiguous=True), reads=[Bmodr], partial=[Btbl])
            for (av, gi, gs) in ((a1, 1, n1g_s), (a2, 3, n2g_s)):
                S.op("dve", lambda e, av=av, gi=gi: e.tensor_scalar_add(out=av[:], in0=tbl[:, gi, :, :], scalar1=1.0),
                     reads=[Btbl], partial=[Btbl])
                S.op("dve", lambda e, av=av, gs=gs: e.tensor_tensor(
                    out=av[:], in0=av[:], in1=gs[:].unsqueeze(1).to_broadcast([128, 5, 8]), op=ALU.mult),
                    reads=[Btbl, Bconst], partial=[Btbl])

            cb = [sb(f"cb{i}", [128, 4096], BF16) for i in range(2)]
            Bcb = bufs(2)
            Bw = {n: Buf() for n in ("w_in", "wq", "wkv", "wo", "wcs", "wout")}
            ctr = [0]

            def cast_store(src_ap, dst_ap, shape, wb, scale_ap=None):
                i = ctr[0]
                ctr[0] += 1
                k, c = i % 3, i % 2
                n = int(np.prod(shape[1:]))
                sv = stg0[k][:, 0:n]
                cv = cb[c][:, 0:n]
                if len(shape) == 3:
                    sv = sv.rearrange("p (a n) -> p a n", a=shape[1])
                    cv = cv.rearrange("p (a n) -> p a n", a=shape[1])
                S.dma("sp", lambda e: e.dma_start(out=sv, in_=src_ap), writes=[Bstg0[k]])
                if scale_ap is None:
                    if i % 2 == 0:
                        S.op("dve", lambda e: e.tensor_copy(out=cv, in_=sv), reads=[Bstg0[k]], writes=[Bcb[c]])
                    else:
                        S.op("act", lambda e: e.activation(out=cv, in_=sv, func=AF.Copy), reads=[Bstg0[k]], writes=[Bcb[c]])
                else:
                    for a in range(shape[1]):
                        S.op("dve", lambda e, a=a: e.tensor_scalar(
                            out=cv[:, a, :], in0=sv[:, a, :], scalar1=scale_ap[:, a:a + 1], scalar2=None, op0=ALU.mult),
                            reads=[Bstg0[k], Bconst], partial=[Bcb[c]])
                S.dma("pool", lambda e: e.dma_start(out=dst_ap, in_=cv), reads=[Bcb[c]], partial=[wb])

            for kc in range(8):
                cast_store(w_in[kc * 128:(kc + 1) * 128, :], w_in_bf[kc * 128:(kc + 1) * 128, :], [128, NIN], Bw["w_in"])
            for k2 in range(2):
                cast_store(w_out[k2 * 512:(k2 + 1) * 512, :].rearrange("(k p) n -> p k n", p=128),
                           wout_bf[k2 * 512:(k2 + 1) * 512, :].rearrange("(k p) n -> p k n", p=128), [128, 4, D], Bw["wout"])
            cast_store(w_o_attn.rearrange("(k p) n -> p k n", p=128), wo_bf.rearrange("(k p) n -> p k n", p=128),
                       [128, 4, D], Bw["wo"])
            qag_s = sb("qag_s", [128, 3], F32)
            kvag_s = sb("kvag_s", [128, 2], F32)
            S.dma("sp", lambda e: e.dma_start(out=qag_s[:], in_=qag), partial=[Bconst])
            S.dma("sp", lambda e: e.dma_start(out=kvag_s[:], in_=kvag), partial=[Bconst])
            cast_store(w_q_up.rearrange("(k p) n -> p k n", p=128), wq_bf.rearrange("(k p) n -> p k n", p=128),
                       [128, 3, 768], Bw["wq"], scale_ap=qag_s)
            cast_store(w_kv_up.rearrange("(k p) n -> p k n", p=128), wkv_bf.rearrange("(k p) n -> p k n", p=128),
                       [128, 2, 1024], Bw["wkv"], scale_ap=kvag_s)
            wf = sb("wf", [128, 4, D], F32)
            chc = sb("chc", [128, 128], F32)
            chs = sb("chs", [128, 128], F32)
            wcs_s = sb("wcs_s", [128, 8, D], BF16)
            Bwf, Bwcs = Buf(), Buf()
            S.dma("sp", lambda e: e.dma_start(out=wf[:], in_=w_fourier.rearrange("(k p) n -> p k n", p=128)), writes=[Bwf])
            S.dma("sp", lambda e: e.dma_start(out=chc[:], in_=ch_c), partial=[Bwf])
            S.dma("sp", lambda e: e.dma_start(out=chs[:], in_=ch_s), partial=[Bwf])
            ii = 0
            for g in range(4):
                for (mi, msb, sgn) in ((0, chc, 1.0 / 512), (1, chs, -1.0 / 512)):
                    for nh in range(2):
                        pb = ii % 2
                        ii += 1
                        mm(PS[pb][:, :], msb[:], wf[:, g, nh * 512:(nh + 1) * 512], True, True, [Bwf], [PSB[pb]])
                        S.op("act", lambda e, pb=pb, g=g, mi=mi, nh=nh, sgn=sgn: e.activation(
                            out=wcs_s[:, mi * 4 + g, nh * 512:(nh + 1) * 512], in_=PS[pb][:, :], func=AF.Copy, scale=sgn),
                            reads=[PSB[pb]], partial=[Bwcs])
            S.dma("pool", lambda e: e.dma_start(out=wcs_bf.rearrange("(k p) n -> p k n", p=128), in_=wcs_s[:]),
                  reads=[Bwcs], writes=[Bw["wcs"]])
        S.barrier()

        ssq_all = sbg("ssq_all", [128, NB, 2, NT + 2], F32)
        Bssq = Buf()
        def phase_A():
            with ExitStack() as st:
                def sb(name, shape, dt):
                    return st.enter_context(nc.sbuf_tensor(name, list(shape), dt))

                NTILE_ALL = NB * (NT + 2)
                w_in_s = sb("w_in_s", [128, 8, NIN], BF16)
                Bwin = Buf()
                for kc in range(8):
                    S.dma("sp", lambda e, kc=kc: e.dma_start(out=w_in_s[:, kc, :], in_=w_in_bf[kc * 128:(kc + 1) * 128, :]),
                          partial=[Bwin])
                xt = [sb(f"xt{i}", [128, D], F32) for i in range(4)]
                xn = [sb(f"xn{i}", [128, D], BF16) for i in range(4)]
                hT = [sb(f"hT{i}", [128, 8, 512], BF16) for i in range(2)]
                junk = sb("junkA", [128, D], BF16)
                ssx = sb("ssx", [128, NTILE_ALL], F32)
                lnx = sb("lnx", [128, NTILE_ALL], F32)
                rsx = sb("rsx", [128, NTILE_ALL], F32)
                sq = [sb(f"sq{i}", [128, 5, 512], BF16) for i in range(2)]
                pq_s = [sb(f"pq_s{i}", [128, 5, 512], BF16) for i in range(2)]
                g_s = [sb(f"g_s{i}", [128, 4, 512], BF16) for i in range(2)]
                F_s = [sb(f"F_s{i}", [128, 512], BF16) for i in range(2)]
                kpe_s = [sb(f"kpe_s{i}", [128, 32], F32) for i in range(2)]
                Bxt, Bxn, BhT, Bpq, Bsq, Bgs, BFs, Bkpe = bufs(4), bufs(4), bufs(2), bufs(2), bufs(2), bufs(2), bufs(2), bufs(2)
                Bjunk, Bssx, Blnx, Brsx = Buf(), Buf(), Buf(), Buf()
                S.op("dve", lambda e: e.memset(ssx[:], 0.0), writes=[Bssx])
                S.op("dve", lambda e: e.memset(ssq_all[:], 1.0), writes=[Bssq])

                lat_chunks = ([("qk", c, OFF_Q + c * 128) for c in range(3)] + [("qk", 3 + c, OFF_KV + c * 128) for c in range(2)]
                              + [("g", c, OFF_GA + c * 128) for c in range(16)])
                ctx_chunks = [("qk", 3 + c, OFF_KV + c * 128) for c in range(2)]
                blocks = []
                col = 0
                for b in range(NB):
                    for tb in range(4):
                        blocks.append(dict(b=b, src=x4[b, tb * 512:(tb + 1) * 512, :], ntile=4, tb=tb, chunks=lat_chunks,
                                           cols=(tb * 512, (tb + 1) * 512), ctx=False, tile0=tb * 4, col0=col))
                        col += 4
                    blocks.append(dict(b=b, src=ctx4[b], ntile=2, tb=0, chunks=ctx_chunks, cols=(0, CTX), ctx=True, tile0=NT, col0=col))
                    col += 2

                pi = 0
                for blk in blocks:
                    for i in range(blk["ntile"]):
                        k = pi % 4
                        pi += 1
                        cidx = blk["col0"] + i
                        S.dma("sp", lambda e, k=k, i=i, blk=blk: e.dma_start(out=xt[k][:], in_=blk["src"][i * 128:(i + 1) * 128, :]),
                              writes=[Bxt[k]])
                        S.op("act", lambda e, k=k, cidx=cidx: e.activation(
                            out=junk[:], in_=xt[k][:], func=AF.Square, accum_out=ssx[:, cidx:cidx + 1]),
                            reads=[Bxt[k], Bssx], writes=[Bjunk], partial=[Bssx])
                S.op("act", lambda e: e.activation(out=lnx[:], in_=ssx[:], func=AF.Ln, scale=1.0 / D, bias=epsb[:, 0:1]),
                     reads=[Bssx, Bconst], writes=[Blnx])
                S.op("act", lambda e: e.activation(out=rsx[:], in_=lnx[:], func=AF.Exp, scale=-0.5), reads=[Blnx], writes=[Brsx])

                cnt = {"fm": 0, "g": 0, "t": 0}

                def P1(n):
                    if n >= len(blocks):
                        return
                    blk = blocks[n]
                    for i in range(blk["ntile"]):
                        cidx = blk["col0"] + i
                        S.dma("sp", lambda e, i=i: e.dma_start(out=xt[i][:], in_=blk["src"][i * 128:(i + 1) * 128, :]), writes=[Bxt[i]])
                        S.op("dve", lambda e, i=i, cidx=cidx: e.tensor_scalar(
                            out=xn[i][:], in0=xt[i][:], scalar1=rsx[:, cidx:cidx + 1], scalar2=None, op0=ALU.mult),
                            reads=[Bxt[i], Brsx], writes=[Bxn[i]])

                def P2_thunks(n):
                    if n >= len(blocks):
                        return []
                    blk = blocks[n]
                    bi = n % 2
                    b = blk["b"]
                    bsel = 4 if blk["ctx"] else b
                    th = []
                    for i in range(blk["ntile"]):
                        def p2a(i=i):
                            ti = cnt["t"]
                            cnt["t"] += 1
                            pt = 6 + (ti % 2)
                            ptv = PS[pt][:].bitcast(BF16).rearrange("p (k t) -> p k t", k=8)
                            for kc in range(8):
                                tr(ptv[:, kc, :], xn[i][:, kc * 128:(kc + 1) * 128], ident_b[:], [Bxn[i], Bconst],
                                   writes=[PSB[pt]] if kc == 0 else (), partial=() if kc == 0 else [PSB[pt]])
                            hv = hT[bi][:, :, i * 128:(i + 1) * 128]
                            S.op("dve", lambda e: e.tensor_tensor(
                                out=hv, in0=ptv, in1=a1[:, bsel, :].unsqueeze(2).to_broadcast([128, 8, 128]), op=ALU.mult),
                                reads=[PSB[pt], Btbl], partial=[BhT[bi]])
                            S.op("dve", lambda e: e.tensor_tensor(
                                out=hv, in0=hv, in1=tbl[:, 0, bsel, :].unsqueeze(2).to_broadcast([128, 8, 128]), op=ALU.add),
                                reads=[Btbl, BhT[bi]], partial=[BhT[bi]])

                        def p2b(i=i):
                            k = i % 2
                            if not blk["ctx"]:
                                pf = i % 2
                                for kc in range(8):
                                    mm(PS[pf][:, :], hT[bi][:, kc, i * 128:(i + 1) * 128], w_in_s[:, kc, OFF_F:OFF_F + 512],
                                       kc == 0, kc == 7, [BhT[bi], Bwin], [PSB[pf]])
                                S.op("act", lambda e: e.activation(out=F_s[k][:], in_=PS[pf][:, :], func=AF.Copy),
                                     reads=[PSB[pf]], writes=[BFs[k]])
                                r0 = blk["tb"] * 512 + i * 128
                                S.dma("pool", lambda e: e.dma_start(out=Ftm[b, r0:r0 + 128, :], in_=F_s[k][:]), reads=[BFs[k]])
                            for kc in range(8):
                                mm(PS[2][:, 0:32], hT[bi][:, kc, i * 128:(i + 1) * 128], w_in_s[:, kc, OFF_KPE:OFF_KPE + 32],
                                   kc == 0, kc == 7, [BhT[bi], Bwin], [PSB[2]])
                            S.op("dve", lambda e: e.tensor_copy(out=kpe_s[k][:], in_=PS[2][:, 0:32]), reads=[PSB[2]], writes=[Bkpe[k]])
                            r1 = (blk["tile0"] + i) * 128
                            S.dma("pool", lambda e: e.dma_start(out=kpe_d[b, r1:r1 + 128, :], in_=kpe_s[k][:]), reads=[Bkpe[k]])
                        th.append(p2a)
                        th.append(p2b)
                    return th

                def M_thunks(n):
                    if n < 0 or n >= len(blocks):
                        return []
                    blk = blocks[n]
                    bi = n % 2
                    b = blk["b"]
                    ntile = blk["ntile"]
                    ntok = ntile * 128
                    cols = blk["cols"]
                    th = []
                    for (kind, cidx, col0) in blk["chunks"]:
                        def chunk(kind=kind, cidx=cidx, col0=col0):
                            pf = 3 + (cnt["fm"] % 2)
                            cnt["fm"] += 1
                            for kc in range(8):
                                mm(PS[pf][:, 0:ntok], w_in_s[:, kc, col0:col0 + 128], hT[bi][:, kc, 0:ntok],
                                   kc == 0, kc == 7, [BhT[bi], Bwin], [PSB[pf]])
                            if kind == "qk":
                                S.op("act", lambda e: e.activation(out=pq_s[bi][:, cidx, 0:ntok], in_=PS[pf][:, 0:ntok], func=AF.Copy),
                                     reads=[PSB[pf]], partial=[Bpq[bi]])
                                S.op("act", lambda e: e.activation(out=sq[bi][:, cidx, 0:ntok], in_=PS[pf][:, 0:ntok], func=AF.Square),
                                     reads=[PSB[pf]], partial=[Bsq[bi]])
                            else:
                                gi = cnt["g"]
                                cnt["g"] += 1
                                gs_i = (gi // 4) % 2
                                S.op("act", lambda e: e.activation(out=g_s[gs_i][:, gi % 4, :], in_=PS[pf][:, :], func=AF.Sigmoid),
                                     reads=[PSB[pf]], partial=[Bgs[gs_i]])
                                if gi % 4 == 3:
                                    c0 = cidx - 3
                                    S.dma("pool", lambda e: e.dma_start(
                                        out=gT[b, c0:c0 + 4, :, cols[0]:cols[1]].rearrange("c p t -> p c t"), in_=g_s[gs_i][:]),
                                        reads=[Bgs[gs_i]])
                        th.append(chunk)

                    def stats():
                        groups = [(0, 3, 0), (3, 5, 1)] if not blk["ctx"] else [(3, 5, 1)]
                        for (c0, c1, which) in groups:
                            for i in range(ntile):
                                for c in range(c0, c1):
                                    mm(PS[5][:, i:i + 1], sq[bi][:, c, i * 128:(i + 1) * 128], ones_b[:, 0:1],
                                       c == c0, c == c1 - 1, [Bsq[bi], Bconst], [PSB[5]])
                            t0 = blk["tile0"]
                            S.op("dve", lambda e, which=which: e.tensor_copy(out=ssq_all[:, b, which, t0:t0 + ntile], in_=PS[5][:, 0:ntile]),
                                 reads=[PSB[5]], partial=[Bssq])
                        if not blk["ctx"]:
                            S.dma("pool", lambda e: e.dma_start(
                                out=pqkvT[b, :, :, cols[0]:cols[1]].rearrange("c p t -> p c t"), in_=pq_s[bi][:]), reads=[Bpq[bi]])
                        else:
                            S.dma("pool", lambda e: e.dma_start(
                                out=pkvcT[b].rearrange("c p t -> p c t"), in_=pq_s[bi][:, 3:5, 0:CTX]), reads=[Bpq[bi]])
                    th.append(stats)
                    return th

                def interleave(a, bq):
                    na, nb_ = len(a), len(bq)
                    j = 0
                    for i, f in enumerate(a):
                        f()
                        while j < nb_ and (j + 1) * na <= (i + 1) * nb_ * 1.0001 + 1e-9:
                            bq[j]()
                            j += 1
                    while j < nb_:
                        bq[j]()
                        j += 1

                P1(0)
                for f in P2_thunks(0):
                    f()
                for n in range(len(blocks)):
                    P1(n + 1)
                    interleave(M_thunks(n), P2_thunks(n + 1))
                lnq = sb("lnq", [128, NB, 2, NT + 2], F32)
                Blnq = Buf()
                for which, nfeat in ((0, 384), (1, 256)):
                    S.op("act", lambda e, which=which, nfeat=nfeat: e.activation(
                        out=lnq[:, :, which, :], in_=ssq_all[:, :, which, :], func=AF.Ln, scale=1.0 / nfeat, bias=epsb[:, 0:1]),
                        reads=[Bssq, Bconst], partial=[Blnq])
                S.op("act", lambda e: e.activation(out=rstd_qkv[:], in_=lnq[:], func=AF.Exp, scale=-0.5), reads=[Blnq], writes=[Brstd])
            S.barrier()

        if run("A"):
            phase_A()

        def phase_B():
            with ExitStack() as st:
                def sb(name, shape, dt):
                    return st.enter_context(nc.sbuf_tensor(name, list(shape), dt))

                wq_s = sb("wq_s", [128, 3, 768], BF16)
                wkv_s = sb("wkv_s", [128, 2, 1024], BF16)
                qg_s = sb("qg_s", [128, QKD], F32)
                kg_s = sb("kg_s", [128, QKD], F32)
                cos_s = sb("cos_s", [128, NT, 32], F32)
                sin_s = sb("sin_s", [128, NT, 32], F32)
                BwB = Buf()
                S.dma("sp", lambda e: e.dma_start(out=wq_s[:], in_=wq_bf.rearrange("(k p) n -> p k n", p=128)), partial=[BwB])
                S.dma("sp", lambda e: e.dma_start(out=wkv_s[:], in_=wkv_bf.rearrange("(k p) n -> p k n", p=128)), partial=[BwB])
                S.dma("sp", lambda e: e.dma_start(out=qg_s[:], in_=qg_bc), partial=[BwB])
                S.dma("sp", lambda e: e.dma_start(out=kg_s[:], in_=kg_bc), partial=[BwB])
                S.dma("sp", lambda e: e.dma_start(out=cos_s[:], in_=rope_cos), partial=[BwB])
                S.dma("sp", lambda e: e.dma_start(out=sin_s[:], in_=rope_sin), partial=[BwB])
                S.op("dve", lambda e: e.tensor_scalar(out=qg_s[:], in0=qg_s[:], scalar1=float(QKD ** -0.5), scalar2=None,
                                                       op0=ALU.mult), reads=[BwB], writes=[BwB])
                pq_in = [sb(f"pq_in{i}", [128, 3, T], BF16) for i in range(2)]
                pkv_in = [sb(f"pkv_in{i}", [128, 2, T + CTX], BF16) for i in range(2)]
                kpe_in = [sb(f"kpe_in{i}", [128, NT + 2, 32], F32) for i in range(2)]
                qT_all = sb("qT_all", [128, H, T], BF16)
                kT_all = sb("kT_all", [128, H, T + CTX], BF16)
                Bpqin, Bpkvin, Bkpein = bufs(2), bufs(2), bufs(2)
                BqT, BkT = Buf(), Buf()
                W = 5
                xf = [sb(f"xf{i}", [128, H, QKD], F32) for i in range(W)]
                sqf = [sb(f"sqf{i}", [128, H, QKD], F32) for i in range(W)]
                ssh = [sb(f"ssh{i}", [128, H], F32) for i in range(W)]
                lnh = [sb(f"lnh{i}", [128, H], F32) for i in range(W)]
                rsh = [sb(f"rsh{i}", [128, H], F32) for i in range(W)]
                t1 = [sb(f"t1_{i}", [128, H, 32], F32) for i in range(W)]
                t2 = [sb(f"t2_{i}", [128, H, 32], F32) for i in range(W)]
                xb = [sb(f"xb{i}", [128, H, QKD], BF16) for i in range(W)]
                vx = [sb(f"vx{i}", [128, H, 65], BF16) for i in range(W)]
                Bxf, Bsqf, Bssh, Blnh, Brsh, Bt1, Bt2, Bxb, Bvx = (bufs(W) for _ in range(9))
                for i in range(W):
                    S.op("pool", lambda e, i=i: e.memset(vx[i][:], 1.0), writes=[Bvx[i]])
                pc = {"n": 0, "b": 0}

                def task(kind, b, ib, i, k):
                    bp = pc["b"] % 3
                    pc["b"] += 1
                    banks = (2 * bp, 2 * bp + 1)
                    xv = xf[k]
                    if kind == "k":
                        width, nk, gain, dstT, dBuf = 128, 64, kg_s, kT_all, BkT
                        rs_ap = rstd_qkv[:, b, 1, i:i + 1]
                        rope_tile = i if i < NT else None
                        for nh in range(2):
                            for c in range(2):
                                mm(PS[banks[nh]][:, :], pkv_in[ib][:, c, i * 128:(i + 1) * 128], wkv_s[:, c, nh * 512:(nh + 1) * 512],
                                   c == 0, c == 1, [Bpkvin[ib], BwB], [PSB[banks[nh]]])
                    else:
                        width, nk, gain, dstT, dBuf = QKD, QKD, qg_s, qT_all, BqT
                        rs_ap = rstd_qkv[:, b, 0, i:i + 1]
                        rope_tile = i
                        for nh in range(2):
                            for c in range(3):
                                mm(PS[banks[nh]][:, 0:384], pq_in[ib][:, c, i * 128:(i + 1) * 128], wq_s[:, c, nh * 384:(nh + 1) * 384],
                                   c == 0, c == 2, [Bpqin[ib], BwB], [PSB[banks[nh]]])
                    for nh in range(2):
                        pv = PS[banks[nh]][:, 0:4 * width].rearrange("p (h w) -> p h w", h=4)
                        S.op("act", lambda e, pv=pv, nh=nh: e.activation(
                            out=xv[:, nh * 4:(nh + 1) * 4, 0:nk], in_=pv[:, :, 0:nk], func=AF.Copy, scale=rs_ap),
                            reads=[PSB[banks[nh]], Brstd], partial=[Bxf[k]])
                        if kind == "k":
                            S.op("act", lambda e, pv=pv, nh=nh: e.activation(
                                out=vx[k][:, nh * 4:(nh + 1) * 4, 0:64], in_=pv[:, :, 64:128], func=AF.Copy, scale=rs_ap),
                                reads=[PSB[banks[nh]], Brstd], partial=[Bvx[k]])
                    yield
                    if kind == "k":
                        kpe_ap = kpe_in[ib][:, i, :]
                        S.op("pool", lambda e: e.tensor_copy(out=xv[:, :, 64:96], in_=kpe_ap.unsqueeze(1).to_broadcast([128, H, 32])),
                             reads=[Bkpein[ib]], partial=[Bxf[k]])
                        S.dma("sp", lambda e: e.dma_start(
                            out=vx_d[b, i * 128:(i + 1) * 128, :], in_=vx[k][:].rearrange("p h w -> p (h w)")), reads=[Bvx[k]])
                        yield
                    S.op("pool", lambda e: e.tensor_tensor(out=sqf[k][:], in0=xv[:], in1=xv[:], op=ALU.mult),
                         reads=[Bxf[k]], writes=[Bsqf[k]])
                    yield
                    S.op("dve", lambda e: e.tensor_reduce(out=ssh[k][:], in_=sqf[k][:], axis=AX.X, op=ALU.add),
                         reads=[Bsqf[k]], writes=[Bssh[k]])
                    yield
                    S.op("act", lambda e: e.activation(out=lnh[k][:], in_=ssh[k][:], func=AF.Ln, scale=1.0 / QKD, bias=epsb[:, 0:1]),
                         reads=[Bssh[k], Bconst], writes=[Blnh[k]])
                    yield
                    S.op("act", lambda e: e.activation(out=rsh[k][:], in_=lnh[k][:], func=AF.Exp, scale=-0.5),
                         reads=[Blnh[k]], writes=[Brsh[k]])
                    yield
                    S.op("dve", lambda e: e.tensor_tensor(out=xv[:], in0=xv[:], in1=rsh[k][:].unsqueeze(2).to_broadcast([128, H, QKD]),
                                                           op=ALU.mult), reads=[Brsh[k]], writes=[Bxf[k]])
                    yield
                    S.op("dve", lambda e: e.tensor_tensor(out=xv[:], in0=xv[:], in1=gain[:].unsqueeze(1).to_broadcast([128, H, QKD]),
                                                           op=ALU.mult), reads=[BwB], writes=[Bxf[k]])
                    yield
                    S.op("act", lambda e: e.activation(out=xb[k][:, :, 0:64], in_=xv[:, :, 0:64], func=AF.Copy),
                         reads=[Bxf[k]], partial=[Bxb[k]])
                    if rope_tile is not None:
                        pe4 = xv[:, :, 64:96].rearrange("p h (g u) -> p h g u", g=2)
                        cs = cos_s[:, rope_tile, :].unsqueeze(1).to_broadcast([128, H, 32])
                        sn4 = sin_s[:, rope_tile, :].rearrange("p (g u) -> p g u", g=2)
                        t24 = t2[k][:].rearrange("p h (g u) -> p h g u", g=2)
                        S.op("dve", lambda e: e.tensor_tensor(out=t1[k][:], in0=xv[:, :, 64:96], in1=cs, op=ALU.mult),
                             reads=[Bxf[k], BwB], writes=[Bt1[k]])
                        S.op("pool", lambda e: e.tensor_tensor(
                            out=t24[:, :, :, 0:8], in0=pe4[:, :, :, 8:16],
                            in1=sn4[:, :, 0:8].unsqueeze(1).to_broadcast([128, H, 2, 8]), op=ALU.mult),
                            reads=[Bxf[k], BwB], writes=[Bt2[k]])
                        yield
                        S.op("pool", lambda e: e.tensor_tensor(
                            out=t24[:, :, :, 8:16], in0=pe4[:, :, :, 0:8],
                            in1=sn4[:, :, 8:16].unsqueeze(1).to_broadcast([128, H, 2, 8]), op=ALU.mult),
                            reads=[Bxf[k], BwB], partial=[Bt2[k]])
                        yield
                        S.op("dve", lambda e: e.tensor_tensor(out=xb[k][:, :, 64:96], in0=t1[k][:], in1=t2[k][:], op=ALU.add),
                             reads=[Bt1[k], Bt2[k]], partial=[Bxb[k]])
                    else:
                        S.op("dve", lambda e: e.tensor_copy(out=xb[k][:, :, 64:96], in_=xv[:, :, 64:96]),
                             reads=[Bxf[k]], partial=[Bxb[k]])
                    yield
                    pt = 6 + (pc["n"] % 2)
                    pc["n"] += 1
                    ptv = PS[pt][:].bitcast(BF16).rearrange("p (h t) -> p h t", h=H)
                    for h in range(H):
                        tr(ptv[0:QKD, h, :], xb[k][:, h, :], ident_b[:], [Bxb[k], Bconst],
                           writes=[PSB[pt]] if h == 0 else (), partial=() if h == 0 else [PSB[pt]])
                    col0 = i * 128
                    S.op("act", lambda e: e.activation(out=dstT[0:QKD, :, col0:col0 + 128], in_=ptv[0:QKD, :, :], func=AF.Copy),
                         reads=[PSB[pt]], partial=[dBuf])

                def run_window(tasks):
                    slots = [None] * W
                    ti = 0
                    while True:
                        for k in range(W):
                            if slots[k] is None and ti < len(tasks):
                                kind, b, ib, i = tasks[ti]
                                ti += 1
                                slots[k] = task(kind, b, ib, i, k)
                        if all(sl is None for sl in slots):
                            break
                        for k in range(W):
                            if slots[k] is not None:
                                try:
                                    next(slots[k])
                                except StopIteration:
                                    slots[k] = None

                def load_b(b):
                    if b >= NB:
                        return
                    ib = b % 2
                    S.dma("sp", lambda e: e.dma_start(out=pq_in[ib][:], in_=pqkvT[b, 0:3].rearrange("c p t -> p c t")),
                          writes=[Bpqin[ib]])
                    S.dma("sp", lambda e: e.dma_start(out=pkv_in[ib][:, :, 0:T], in_=pqkvT[b, 3:5].rearrange("c p t -> p c t")),
                          writes=[Bpkvin[ib]])
                    S.dma("sp", lambda e: e.dma_start(out=pkv_in[ib][:, :, T:T + CTX], in_=pkvcT[b].rearrange("c p t -> p c t")),
                          partial=[Bpkvin[ib]])
                    S.dma("sp", lambda e: e.dma_start(out=kpe_in[ib][:], in_=kpe_d[b].rearrange("(i p) f -> p i f", p=128)),
                          writes=[Bkpein[ib]])

                load_b(0)
                for b in range(NB):
                    ib = b % 2
                    load_b(b + 1)
                    tasks = []
                    for i in range(NT + 2):
                        tasks.append(("k", b, ib, i))
                        if i < NT:
                            tasks.append(("q", b, ib, i))
                    run_window(tasks)
                    S.dma("sp", lambda e, b=b: e.dma_start(out=qT_d[b].rearrange("h d t -> d h t"), in_=qT_all[0:QKD]),
                          reads=[BqT])
                    S.dma("sp", lambda e, b=b: e.dma_start(out=kT_d[b].rearrange("h d t -> d h t"), in_=kT_all[0:QKD]),
                          reads=[BkT])
            S.barrier()

        if run("B"):
            phase_B()

        def phase_C():
            with ExitStack() as st:
                def sb(name, shape, dt):
                    return st.enter_context(nc.sbuf_tensor(name, list(shape), dt))

                NKC = (T + CTX) // 128
                qTs = [sb(f"qTs{i}", [128, T], BF16) for i in range(3)]
                kTs = [sb(f"kTs{i}", [128, T + CTX], BF16) for i in range(3)]
                vxs = [sb(f"vxs{i}", [128, NKC, H * 65], BF16) for i in range(2)]
                pT = [sb(f"pT{i}", [128, 1024], BF16) for i in range(3)]
                attn_s = sb("attn_s", [128, NT, 512], BF16)
                attnT_s = sb("attnT_s", [128, 4, T], BF16)
                rsum = sb("rsum", [128, 4], F32)
                Bq, Bk, Bv, BpT = bufs(3), bufs(3), bufs(2), bufs(3)
                Battn, BattnT, Brsum = Buf(), Buf(), Buf()
                for i in range(3):
                    S.op("pool", lambda e, i=i: e.memset(qTs[i][:], 0.0), writes=[Bq[i]])
                    S.op("pool", lambda e, i=i: e.memset(kTs[i][:], 0.0), writes=[Bk[i]])
                items = [(b, h, qb, kp) for b in range(NB) for h in range(H) for qb in range(4) for kp in range(NKC // 2)]
                heads = [(b, h) for b in range(NB) for h in range(H)]

                def load_head(n):
                    if n >= len(heads):
                        return
                    b, h = heads[n]
                    ih = n % 3
                    S.dma("sp", lambda e: e.dma_start(out=qTs[ih][0:QKD, :], in_=qT_d[b, h]), writes=[Bq[ih]])
                    S.dma("sp", lambda e: e.dma_start(out=kTs[ih][0:QKD, :], in_=kT_d[b, h]), writes=[Bk[ih]])

                def load_v(b):
                    if b >= NB:
                        return
                    iv = b % 2
                    S.dma("sp", lambda e: e.dma_start(out=vxs[iv][:], in_=vx_d[b].rearrange("(c p) w -> p c w", p=128)),
                          writes=[Bv[iv]])

                load_v(0)
                load_head(0)

                def st_qk(n):
                    b, h, qb, kp = items[n]
                    hn = b * H + h
                    ih = hn % 3
                    if qb == 0 and kp == 0:
                        load_head(hn + 1)
                        if h == 0:
                            load_v(b + 1)
                    pr = n % 2
                    for u in range(2):
                        kc = 2 * kp + u
                        mm(PS[2 * pr + u], kTs[ih][:, kc * 128:(kc + 1) * 128], qTs[ih][:, qb * 512:(qb + 1) * 512],
                           True, True, [Bq[ih], Bk[ih]], [PSB[2 * pr]] if u == 0 else (), () if u == 0 else [PSB[2 * pr]])

                def st_exp(n):
                    pr, ip = n % 2, n % 3
                    S.op("act", lambda e: e.activation(out=pT[ip][:], in_=PSBIG[:, pr * 1024:(pr + 1) * 1024], func=AF.Exp),
                         reads=[PSB[2 * pr]], writes=[BpT[ip]])

                def st_pv(n):
                    b, h, qb, kp = items[n]
                    ip, iv = n % 3, b % 2
                    for u in range(2):
                        kc = 2 * kp + u
                        for j in range(4):
                            mm(PS[4 + j][:, 0:65], pT[ip][:, u * 512 + j * 128:u * 512 + (j + 1) * 128],
                               vxs[iv][:, kc, h * 65:(h + 1) * 65], kc == 0, kc == NKC - 1, [BpT[ip], Bv[iv]], [PSB[4 + j]])
                    if kp != NKC // 2 - 1:
                        return
                    ov = PSBIG[:, 2048:4096].rearrange("p (j c) -> p j c", j=4)
                    S.op("dve", lambda e: e.reciprocal(out=rsum[:], in_=ov[:, :, 64]),
                         reads=[PSB[4], PSB[5], PSB[6], PSB[7]], writes=[Brsum])
                    S.op("dve", lambda e: e.tensor_tensor(
                        out=attn_s[:, qb * 4:(qb + 1) * 4, h * 64:(h + 1) * 64], in0=ov[:, :, 0:64],
                        in1=rsum[:].unsqueeze(2).to_broadcast([128, 4, 64]), op=ALU.mult),
                        reads=[PSB[4], PSB[5], PSB[6], PSB[7], Brsum], partial=[Battn])
                    if not (h == H - 1 and qb == 3):
                        return
                    for i in range(NT):
                        pt = 4 + (i % 4)
                        ptv = PS[pt][:].bitcast(BF16).rearrange("p (c t) -> p c t", c=8)
                        for c in range(4):
                            tr(ptv[:, c, :], attn_s[:, i, c * 128:(c + 1) * 128], ident_b[:], [Battn, Bconst],
                               writes=[PSB[pt]] if c == 0 else (), partial=() if c == 0 else [PSB[pt]])
                        S.op("act", lambda e, ptv=ptv, i=i: e.activation(out=attnT_s[:, :, i * 128:(i + 1) * 128], in_=ptv[:, 0:4, :],
                                                                            func=AF.Copy), reads=[PSB[pt]], partial=[BattnT])
                    S.dma("pool", lambda e: e.dma_start(out=attnT[b].rearrange("c p t -> p c t"), in_=attnT_s[:]),
                          reads=[BattnT])

                nit = len(items)
                for s_ in range(nit + 2):
                    if s_ < nit:
                        st_qk(s_)
                    if 0 <= s_ - 1 < nit:
                        st_exp(s_ - 1)
                    if 0 <= s_ - 2 < nit:
                        st_pv(s_ - 2)
            S.barrier()

        if run("C"):
            phase_C()

        def phase_E():
            with ExitStack() as st:
                def sb(name, shape, dt):
                    return st.enter_context(nc.sbuf_tensor(name, list(shape), dt))

                wo_s = sb("wo_s", [128, 4, D], BF16)
                wcs_s2 = sb("wcs_s2", [128, 8, D], BF16)
                wout_s = sb("wout_s", [128, 8, D], BF16)
                wr_f = sb("wr_f", [128, 8, NE], F32)
                wr_s = sb("wr_s", [128, 8, NE], BF16)
                n2bc = sb("n2bc", [128, D], F32)
                BwE = Buf()
                S.dma("sp", lambda e: e.dma_start(out=wo_s[:], in_=wo_bf.rearrange("(k p) n -> p k n", p=128)), partial=[BwE])
                S.dma("sp", lambda e: e.dma_start(out=wcs_s2[:], in_=wcs_bf.rearrange("(k p) n -> p k n", p=128)), partial=[BwE])
                S.dma("sp", lambda e: e.dma_start(out=wout_s[:], in_=wout_bf.rearrange("(k p) n -> p k n", p=128)), partial=[BwE])
                S.dma("sp", lambda e: e.dma_start(out=wr_f[:], in_=w_router.rearrange("(k p) n -> p k n", p=128)), partial=[BwE])
                S.dma("sp", lambda e: e.dma_start(out=n2bc[:], in_=n2g_bc), partial=[BwE])
                S.op("dve", lambda e: e.tensor_copy(out=wr_s[:], in_=wr_f[:]), reads=[BwE], partial=[BwE])
                Fs = [sb("Fs0", [128, NT, 512], BF16)] * 2
                dblk = [sb(f"dblk{i}", [128, NT, 512], BF16) for i in range(2)]
                uvt = [sb("uvt0", [128, 8, 512], BF16)] * 2
                gblk = [sb("gblk0", [128, 16, 512], BF16)] * 2
                ablk = [sb(f"ablk{i}", [128, 4, 512], BF16) for i in range(2)]
                mT = [sb("mT0", [128, 8, 512], BF16)] * 2
                tA = [sb(f"tA{i}", [128, 512], F32) for i in range(2)]
                tB = [sb(f"tB{i}", [128, 512], F32) for i in range(2)]
                g1bc = [sb(f"g1bc{i}", [128, D], F32) for i in range(2)]
                a2bc = [sb(f"a2bc{i}", [128, D], F32) for i in range(2)]
                s2bc = [sb(f"s2bc{i}", [128, D], F32) for i in range(2)]
                xt = [sb(f"xtE{i}", [128, D], F32) for i in range(2)]
                x1 = [sb(f"x1_{i}", [128, D], F32) for i in range(2)]
                tt = sb("ttE", [128, D], F32)
                tt2 = sb("ttE2", [128, D], F32)
                junk = sb("junkE", [128, D], BF16)
                h2s = [sb(f"h2s{i}", [128, D], BF16) for i in range(2)]
                h2T = [sb(f"h2T{i}", [128, 8, 128], BF16) for i in range(2)]
                ss2 = sb("ss2", [128, NB * NT], F32)
                ln2 = sb("ln2", [128, NB * NT], F32)
                rs2 = sb("rs2", [128, NB * NT], F32)
                mx = sb("mx", [128, 2], F32)
                ex = sb("ex", [128, NE], F32)
                sm = sb("sm", [128, 2], F32)
                BFs2, Bd, Buv, Bg, Ba, BmT, BtA, BtB = [Buf()] * 2, bufs(2), [Buf()] * 2, [Buf()] * 2, bufs(2), [Buf()] * 2, bufs(2), bufs(2)
                Bg1, Ba2, Bs2, BxtE, Bx1, Bh2s, Bh2T = bufs(2), bufs(2), bufs(2), bufs(2), bufs(2), bufs(2), bufs(2)
                Btt, BjE, Bss2, Bln2, Brs2, Bmx, Bex, Bsm, Btt2 = Buf(), Buf(), Buf(), Buf(), Buf(), Buf(), Buf(), Buf(), Buf()
                S.op("dve", lambda e: e.memset(ss2[:], 0.0), writes=[Bss2])
                S.op("dve", lambda e: e.memset(sm[:], 0.0), writes=[Bsm])
                cE = {"d": 0, "p": 0, "t": 0}

                def load_batch(b):
                    ib = b % 2
                    S.dma("sp", lambda e: e.dma_start(out=Fs[ib][:], in_=Ftm[b].rearrange("(c p) f -> p c f", p=128)),
                          writes=[BFs2[ib]])
                    S.dma("sp", lambda e: e.dma_start(out=g1bc[ib][:], in_=modr[b, 2 * D:3 * D].partition_broadcast(128)),
                          writes=[Bg1[ib]])
                    S.dma("sp", lambda e: e.dma_start(out=a2bc[ib][:], in_=modr[b, 4 * D:5 * D].partition_broadcast(128)),
                          writes=[Ba2[ib]])
                    S.dma("sp", lambda e: e.dma_start(out=s2bc[ib][:], in_=modr[b, 3 * D:4 * D].partition_broadcast(128)),
                          writes=[Bs2[ib]])
                    S.op("dve", lambda e: e.tensor_scalar_add(out=a2bc[ib][:], in0=a2bc[ib][:], scalar1=1.0),
                         reads=[Ba2[ib]], writes=[Ba2[ib]])
                    S.op("dve", lambda e: e.tensor_mul(out=a2bc[ib][:], in0=a2bc[ib][:], in1=n2bc[:]),
                         reads=[Ba2[ib], BwE], writes=[Ba2[ib]])

                def fourier(g):
                    b, tb = divmod(g, 4)
                    ib, bi = b % 2, g % 2
                    cols = (tb * 512, (tb + 1) * 512)
                    if tb == 0:
                        load_batch(b)
                    S.dma("sp", lambda e: e.dma_start(
                        out=gblk[bi][:], in_=gT[b, :, :, cols[0]:cols[1]].rearrange("c p t -> p c t")), writes=[Bg[bi]])
                    S.dma("sp", lambda e: e.dma_start(
                        out=ablk[bi][:], in_=attnT[b, :, :, cols[0]:cols[1]].rearrange("c p t -> p c t")), writes=[Ba[bi]])
                    for mi in range(2):
                        di = mi
                        for fc in range(4):
                            pb = cE["p"] % 2
                            cE["p"] += 1
                            for c in range(NT):
                                mm(PS[pb][:, :], Fs[ib][:, c, fc * 128:(fc + 1) * 128], dblk[di][:, c, :], c == 0, c == NT - 1,
                                   [BFs2[ib], Bd[di]], [PSB[pb]])
                            S.op("act", lambda e, pb=pb, mi=mi, fc=fc: e.activation(
                                out=uvt[bi][:, mi * 4 + fc, :], in_=PS[pb][:, :], func=AF.Copy), reads=[PSB[pb]], partial=[Buv[bi]])
                            if fc == 3:
                                load_dft(g + 1, mi)
                            yield

                def load_dft(g, mi):
                    if g >= NB * 4:
                        return
                    tb = g % 4
                    dm = (dft_c, dft_s)[mi]
                    S.dma("sp", lambda e: e.dma_start(
                        out=dblk[mi][:], in_=dm[:, tb * 512:(tb + 1) * 512].rearrange("(c p) t -> p c t", p=128)), writes=[Bd[mi]])

                def merge(g):
                    bi = g % 2
                    for n_ in range(8):
                        pa, pf = 2 + (n_ % 2) * 2, 3 + (n_ % 2) * 2
                        for kc in range(4):
                            mm(PS[pa][:, :], wo_s[:, kc, n_ * 128:(n_ + 1) * 128], ablk[bi][:, kc, :], kc == 0, kc == 3,
                               [BwE, Ba[bi]], [PSB[pa]])
                        for kc in range(8):
                            mm(PS[pf][:, :], wcs_s2[:, kc, n_ * 128:(n_ + 1) * 128], uvt[bi][:, kc, :], kc == 0, kc == 7,
                               [BwE, Buv[bi]], [PSB[pf]])
                        k2 = n_ % 2
                        S.op("dve", lambda e, pa=pa, k2=k2, n_=n_: e.tensor_tensor(
                            out=tA[k2][:], in0=PS[pa][:, :], in1=gblk[bi][:, n_, :], op=ALU.mult),
                            reads=[PSB[pa], Bg[bi]], writes=[BtA[k2]])
                        S.op("dve", lambda e, pf=pf, k2=k2, n_=n_: e.tensor_tensor(
                            out=tB[k2][:], in0=PS[pf][:, :], in1=gblk[bi][:, 8 + n_, :], op=ALU.mult),
                            reads=[PSB[pf], Bg[bi]], writes=[BtB[k2]])
                        S.op("pool", lambda e, k2=k2, n_=n_: e.tensor_tensor(
                            out=mT[bi][:, n_, :], in0=tA[k2][:], in1=tB[k2][:], op=ALU.add),
                            reads=[BtA[k2], BtB[k2]], partial=[BmT[bi]])
                        yield

                def O_tile(g, i):
                    b, tb = divmod(g, 4)
                    ib, bi = b % 2, g % 2
                    ti = tb * 4 + i
                    col = b * NT + ti
                    k = (g * 4 + i) % 2
                    r0 = ti * 128
                    S.dma("sp", lambda e: e.dma_start(out=xt[k][:], in_=x4[b, r0:r0 + 128, :]), writes=[BxtE[k]])
                    for nh in range(2):
                        pw = 2 + nh
                        for kc in range(8):
                            mm(PS[pw][:, :], mT[bi][:, kc, i * 128:(i + 1) * 128], wout_s[:, kc, nh * 512:(nh + 1) * 512],
                               kc == 0, kc == 7, [BmT[bi], BwE], [PSB[pw]])
                        S.op("dve", lambda e, pw=pw, nh=nh: e.tensor_tensor(
                            out=tt[:, nh * 512:(nh + 1) * 512], in0=PS[pw][:, :], in1=g1bc[ib][:, nh * 512:(nh + 1) * 512], op=ALU.mult),
                            reads=[PSB[pw], Bg1[ib]], partial=[Btt])
                    S.op("pool", lambda e: e.tensor_tensor(out=x1[k][:], in0=tt[:], in1=xt[k][:], op=ALU.add),
                         reads=[Btt, BxtE[k]], writes=[Bx1[k]])
                    S.dma("pool", lambda e: e.dma_start(out=out[b, r0:r0 + 128, :], in_=x1[k][:]), reads=[Bx1[k]])
                    S.op("act", lambda e: e.activation(out=junk[:], in_=x1[k][:], func=AF.Square, accum_out=ss2[:, col:col + 1]),
                         reads=[Bx1[k], Bss2], writes=[BjE], partial=[Bss2])
                    S.op("act", lambda e: e.activation(
                        out=ln2[:, col:col + 1], in_=ss2[:, col:col + 1], func=AF.Ln, scale=1.0 / D, bias=epsb[:, 0:1]),
                        reads=[Bss2, Bconst], partial=[Bln2])
                    S.op("act", lambda e: e.activation(
                        out=rs2[:, col:col + 1], in_=ln2[:, col:col + 1], func=AF.Exp, scale=-0.5), reads=[Bln2], partial=[Brs2])
                    S.op("dve", lambda e: e.scalar_tensor_tensor(
                        out=tt2[:], in0=x1[k][:], scalar=rs2[:, col:col + 1], in1=a2bc[ib][:], op0=ALU.mult, op1=ALU.mult),
                        reads=[Bx1[k], Brs2, Ba2[ib]], writes=[Btt2])
                    S.op("pool", lambda e: e.tensor_tensor(out=h2s[k][:], in0=tt2[:], in1=s2bc[ib][:], op=ALU.add),
                         reads=[Btt2, Bs2[ib]], writes=[Bh2s[k]])
                    S.dma("pool", lambda e: e.dma_start(out=h2_d[b, r0:r0 + 128, :], in_=h2s[k][:]), reads=[Bh2s[k]])

                def R_tile(g, i):
                    b, tb = divmod(g, 4)
                    ti = tb * 4 + i
                    k = (g * 4 + i) % 2
                    pt = 6 + (cE["t"] % 2)
                    cE["t"] += 1
                    ptv = PS[pt][:].bitcast(BF16).rearrange("p (k t) -> p k t", k=8)
                    for kc in range(8):
                        tr(ptv[:, kc, :], h2s[k][:, kc * 128:(kc + 1) * 128], ident_b[:], [Bh2s[k], Bconst],
                           writes=[PSB[pt]] if kc == 0 else (), partial=() if kc == 0 else [PSB[pt]])
                    S.op("act", lambda e: e.activation(out=h2T[k][:], in_=ptv, func=AF.Copy), reads=[PSB[pt]], writes=[Bh2T[k]])
                    yield
                    for kc in range(8):
                        mm(PS[5][:, 0:NE], h2T[k][:, kc, :], wr_s[:, kc, :], kc == 0, kc == 7, [Bh2T[k], BwE], [PSB[5]])
                    S.op("dve", lambda e: e.tensor_reduce(out=mx[:, 0:1], in_=PS[5][:, 0:NE], axis=AX.X, op=ALU.max),
                         reads=[PSB[5]], writes=[Bmx])
                    S.op("dve", lambda e: e.tensor_scalar(out=mx[:, 1:2], in0=mx[:, 0:1], scalar1=-1.0, scalar2=None, op0=ALU.mult),
                         reads=[Bmx], writes=[Bmx])
                    S.op("act", lambda e: e.activation(out=ex[:], in_=PS[5][:, 0:NE], func=AF.Exp, bias=mx[:, 1:2]),
                         reads=[PSB[5], Bmx], writes=[Bex])
                    S.op("dve", lambda e: e.tensor_reduce(out=sm[:, 0:1], in_=ex[:], axis=AX.X, op=ALU.add),
                         reads=[Bex], writes=[Bsm])
                    S.op("dve", lambda e: e.reciprocal(out=sm[:, 1:2], in_=sm[:, 0:1]), reads=[Bsm], writes=[Bsm])
                    S.op("dve", lambda e: e.tensor_scalar(
                        out=aff_all[:, ti, b, :], in0=ex[:], scalar1=sm[:, 1:2], scalar2=None, op0=ALU.mult),
                        reads=[Bex, Bsm], partial=[Baff])

                def drain(gen):
                    for _ in gen:
                        pass

                NG = NB * 4
                load_dft(0, 0)
                load_dft(0, 1)
                drain(fourier(0))
                drain(merge(0))
                for g in range(1, NG + 1):
                    fgen = fourier(g) if g < NG else iter(())
                    for kind, i in (("O", 0), ("O", 1), ("R", 0), ("O", 2), ("R", 1), ("O", 3)):
                        next(fgen, None)
                        if kind == "O":
                            O_tile(g - 1, i)
                        else:
                            rg = R_tile(g - 1, i)
                            next(rg, None)
                            next(fgen, None)
                            drain(rg)
                    drain(fgen)
                    mgen = merge(g) if g < NG else iter(())
                    for i in (2, 3):
                        next(mgen, None)
                        next(mgen, None)
                        rg = R_tile(g - 1, i)
                        next(rg, None)
                        next(mgen, None)
                        next(mgen, None)
                        drain(rg)
                    drain(mgen)
            S.barrier()

        if run("E"):
            phase_E()

        def phase_F():
            with ExitStack() as st:
                def sb(name, shape, dt):
                    return st.enter_context(nc.sbuf_tensor(name, list(shape), dt))

                NBE = NB * NE
                iota_f = sb("iota_f", [128, 256], F32)
                iota = sb("iota", [128, 256], BF16)
                tri = sb("tri_s", [128, 128], BF16)
                rvc_s = sb("rvc_s", [128, NT, NE, 2], BF16)
                rv = sb("rv", [128, NT, NB, NE, 4], BF16)
                affT = sb("affT", [NBE, T], F32)
                wk = [sb(f"wk{i}", [NBE, T], F32) for i in range(2)]
                m8 = sb("m8", [NBE, 8], F32)
                maskT = sb("maskT", [NBE, T], F32)
                mask_f = sb("mask_f", [128, NT, NBE], F32)
                mask_b = sb("mask_b", [128, NT, NBE], BF16)
                posm = sb("posm", [128, NT, NBE], F32)
                oh = [sb(f"oh{i}", [128, 256], BF16) for i in range(4)]
                hi_f = sb("hi_f", [128, NT, NB, NE], F32)
                res = [sb(f"res{i}", [128, 2, 4], F32) for i in range(2)]
                idf = [sb(f"idf{i}", [128, 2], F32) for i in range(2)]
                BcF, Brv, BaffT, Bm8, BmaskT, Bmf, Bmb, Bposm, Bhif = (Buf() for _ in range(9))
                Bwk, Boh, Bres, Bidf = bufs(2), bufs(4), bufs(2), bufs(2)
                S.dma("sp", lambda e: e.dma_start(out=iota_f[:], in_=iota_d), partial=[BcF])
                S.dma("sp", lambda e: e.dma_start(out=tri[:], in_=tri_d), partial=[BcF])
                S.dma("sp", lambda e: e.dma_start(out=rvc_s[:], in_=rvc_d), partial=[BcF])
                S.op("dve", lambda e: e.tensor_copy(out=iota[:], in_=iota_f[:]), reads=[BcF], partial=[BcF])
                aff_v = aff_all[:]
                for b in range(NB):
                    S.op("dve", lambda e, b=b: e.tensor_copy(out=rv[:, :, b, :, 0:2], in_=rvc_s[:]), reads=[BcF], partial=[Brv])
                S.op("dve", lambda e: e.tensor_copy(out=rv[:, :, :, :, 2], in_=aff_v), reads=[Baff], partial=[Brv])
                S.op("dve", lambda e: e.tensor_copy(out=hi_f[:], in_=rv[:, :, :, :, 2]), reads=[Brv], writes=[Bhif])
                S.op("dve", lambda e: e.tensor_tensor(out=hi_f[:], in0=aff_v, in1=hi_f[:], op=ALU.subtract),
                     reads=[Baff, Bhif], writes=[Bhif])
                S.op("dve", lambda e: e.tensor_copy(out=rv[:, :, :, :, 3], in_=hi_f[:]), reads=[Bhif], partial=[Brv])
                for g in range(4):
                    pb = g % 2
                    for j in range(4):
                        i = g * 4 + j
                        tr(PS[pb][0:NBE, j * 128:(j + 1) * 128], aff_all[:, i, :, :].rearrange("p b e -> p (b e)"), ident_f[:], [Baff, Bconst],
                           writes=[PSB[pb]] if j == 0 else (), partial=() if j == 0 else [PSB[pb]])
                    S.op("act", lambda e, pb=pb, g=g: e.activation(out=affT[:, g * 512:(g + 1) * 512], in_=PS[pb][0:NBE, :], func=AF.Copy),
                         reads=[PSB[pb]], partial=[BaffT])
                cur, curB = affT, BaffT
                for it in range(CAP // 8):
                    S.op("dve", lambda e, cur=cur: e.max(out=m8[:], in_=cur[:]), reads=[curB], writes=[Bm8])
                    if it < CAP // 8 - 1:
                        nxt, nB = wk[it % 2], Bwk[it % 2]
                        S.op("dve", lambda e, cur=cur, nxt=nxt: e.match_replace(
                            out=nxt[:], in_to_replace=m8[:], in_values=cur[:], imm_value=-1.0), reads=[curB, Bm8], writes=[nB])
                        cur, curB = nxt, nB
                S.op("dve", lambda e: e.tensor_scalar(out=maskT[:], in0=affT[:], scalar1=m8[:, 7:8], scalar2=None, op0=ALU.is_ge),
                     reads=[BaffT, Bm8], writes=[BmaskT])
                for g in range(2):
                    pb = 2 + g
                    for j in range(8):
                        i = g * 8 + j
                        tr(PS[pb][:, j * NBE:(j + 1) * NBE], maskT[:, i * 128:(i + 1) * 128], ident_f[0:NBE, 0:NBE], [BmaskT, Bconst],
                           writes=[PSB[pb]] if j == 0 else (), partial=() if j == 0 else [PSB[pb]])
                    S.op("act", lambda e, pb=pb, g=g: e.activation(
                        out=mask_f[:, g * 8:(g + 1) * 8, :], in_=PS[pb][:, :].rearrange("p (j n) -> p j n", j=8), func=AF.Copy),
                        reads=[PSB[pb]], partial=[Bmf])
                S.op("dve", lambda e: e.tensor_copy(out=mask_b[:], in_=mask_f[:]), reads=[Bmf], writes=[Bmb])
                for i in range(NT):
                    pb = 4 + (i % 2)
                    for j in range(i + 1):
                        mm(PS[pb][:, 0:NBE], ones_b[:] if j < i else tri[:], mask_b[:, j, :], j == 0, j == i,
                           [Bmb, Bconst, BcF], [PSB[pb]])
                    S.op("dve", lambda e, pb=pb, i=i: e.scalar_tensor_tensor(
                        out=posm[:, i, :], in0=PS[pb][:, 0:NBE], scalar=1.0, in1=mask_f[:, i, :], op0=ALU.add, op1=ALU.mult),
                        reads=[PSB[pb], Bmf], partial=[Bposm])
                S.op("dve", lambda e: e.tensor_scalar_add(out=posm[:], in0=posm[:], scalar1=-1.0), reads=[Bposm], writes=[Bposm])
                ohc = 0
                for b in range(NB):
                    for ex_ in range(NE):
                        col = b * NE + ex_
                        r_i = col % 2
                        pbase = 2 * r_i
                        for i in range(NT):
                            io = ohc % 4
                            ohc += 1
                            S.op("dve", lambda e, io=io, i=i, col=col: e.tensor_scalar(
                                out=oh[io][:], in0=iota[:], scalar1=posm[:, i, col:col + 1], scalar2=None, op0=ALU.is_equal),
                                reads=[BcF, Bposm], writes=[Boh[io]])
                            for half in range(2):
                                mm(PS[pbase + half][:, 0:4], oh[io][:, half * 128:(half + 1) * 128], rv[:, i, b, ex_, :], i == 0, i == NT - 1,
                                   [Boh[io], Brv], [PSB[pbase + half]])
                        for half in range(2):
                            S.op("act", lambda e, half=half, r_i=r_i, pbase=pbase: e.activation(
                                out=res[r_i][:, half, :], in_=PS[pbase + half][:, 0:4], func=AF.Copy),
                                reads=[PSB[pbase + half]], partial=[Bres[r_i]])
                        S.op("dve", lambda e, r_i=r_i, b=b: e.tensor_scalar(
                            out=idf[r_i][:], in0=res[r_i][:, :, 0], scalar1=128.0, scalar2=float(b * T), op0=ALU.mult, op1=ALU.add),
                            reads=[Bres[r_i]], writes=[Bidf[r_i]])
                        S.op("dve", lambda e, r_i=r_i: e.tensor_tensor(out=idf[r_i][:], in0=idf[r_i][:], in1=res[r_i][:, :, 1], op=ALU.add),
                             reads=[Bres[r_i], Bidf[r_i]], writes=[Bidf[r_i]])
                        S.op("dve", lambda e, r_i=r_i, b=b, ex_=ex_: e.tensor_copy(out=idx_all[:, b, ex_, :], in_=idf[r_i][:]),
                             reads=[Bidf[r_i]], partial=[Bidx])
                        S.op("dve", lambda e, r_i=r_i, b=b, ex_=ex_: e.tensor_tensor(
                            out=gate_all[:, b, ex_, :], in0=res[r_i][:, :, 2], in1=res[r_i][:, :, 3], op=ALU.add),
                            reads=[Bres[r_i]], partial=[Bidx])
            S.barrier()

        if run("F"):
            phase_F()
        elif "F" in skip:
            S.op("pool", lambda e: e.iota(idx_all[:], pattern=[[0, NB * NE * 2]], base=0, channel_multiplier=1), writes=[Bidx])
            S.op("dve", lambda e: e.memset(gate_all[:], 0.0), partial=[Bidx])

        def phase_G():
            with ExitStack() as st:
                def sb(name, shape, dt):
                    return st.enter_context(nc.sbuf_tensor(name, list(shape), dt))

                stg = [sb(f"stgG{i}", [128, 4096], F32) for i in range(3)]
                wg = [sb(f"wg{i}", [128, 8, 512], BF16) for i in range(2)]
                wu = [sb(f"wu{i}", [128, 8, 512], BF16) for i in range(2)]
                wd = [sb(f"wd{i}", [128, 4, D], BF16) for i in range(2)]
                g2bc = [sb(f"g2bc{i}", [128, D], F32) for i in range(NB)]
                xe = [sb(f"xe{i}", [128, D], BF16) for i in range(6)]
                xeT = [sb(f"xeT{i}", [128, 8, 256], BF16) for i in range(2)]
                sg = [sb(f"sg{i}", [128, 256], F32) for i in range(2)]
                actT = [sb(f"actT{i}", [128, 4, 256], BF16) for i in range(2)]
                yo = [sb(f"yo{i}", [128, D], F32) for i in range(3)]
                Bstg, Bwg, Bwu, Bwd = bufs(3), bufs(2), bufs(2), bufs(2)
                Bg2, Bxe, BxeT, Bsg, BactT, Byo = bufs(NB), bufs(6), bufs(2), bufs(2), bufs(2), bufs(3)
                Bout = bufs(NB)
                for b in range(NB):
                    S.dma("sp", lambda e, b=b: e.dma_start(out=g2bc[b][:], in_=modr[b, 5 * D:6 * D].partition_broadcast(128)),
                          writes=[Bg2[b]])
                steps = [(ex_, b) for ex_ in range(NE) for b in range(NB)]
                NS = len(steps)
                wsrc = lambda ex_: ((w_e_gate[ex_], wg[ex_ % 2], Bwg[ex_ % 2], 8), (w_e_up[ex_], wu[ex_ % 2], Bwu[ex_ % 2], 8),
                                    (w_e_down[ex_], wd[ex_ % 2], Bwd[ex_ % 2], 4))
                cc_ = {"y": 0, "p": 0, "t": 0}

                def w_load(ex_):
                    if ex_ >= NE:
                        return
                    for j, (src, dst, dB, a) in enumerate(wsrc(ex_)):
                        sv = stg[j][:].rearrange("p (a n) -> p a n", a=a)
                        S.dma("sp", lambda e, sv=sv, src=src: e.dma_start(out=sv, in_=src.rearrange("(a p) n -> p a n", p=128)),
                              writes=[Bstg[j]])

                def w_cast(ex_, j):
                    if ex_ >= NE:
                        return
                    src, dst, dB, a = wsrc(ex_)[j]
                    sv = stg[j][:].rearrange("p (a n) -> p a n", a=a)
                    if j < 2:
                        S.op("act", lambda e: e.activation(out=dst[:], in_=sv, func=AF.Copy), reads=[Bstg[j]], writes=[dB])
                    else:
                        S.op("dve", lambda e: e.tensor_copy(out=dst[:], in_=sv), reads=[Bstg[j]], writes=[dB])

                def S1(s_):
                    if s_ >= NS:
                        return
                    ex_, b = steps[s_]
                    for half in range(2):
                        kx = (2 * s_ + half) % 6
                        S.dma("pool", lambda e, kx=kx, half=half: e.indirect_dma_start(
                            out=xe[kx][:], out_offset=None, in_=h2_d.rearrange("b t d -> (b t) d"),
                            in_offset=bass.IndirectOffsetOnAxis(ap=idx_all[:, b, ex_, half:half + 1], axis=0)),
                            reads=[Bidx], writes=[Bxe[kx]])

                def S2(s_, half):
                    if s_ >= NS:
                        return
                    kx = (2 * s_ + half) % 6
                    ix = s_ % 2
                    pt = 6 + (cc_["t"] % 2)
                    cc_["t"] += 1
                    ptv = PS[pt][:].bitcast(BF16).rearrange("p (k t) -> p k t", k=8)
                    for kc in range(8):
                        tr(ptv[:, kc, :], xe[kx][:, kc * 128:(kc + 1) * 128], ident_b[:], [Bxe[kx], Bconst],
                           writes=[PSB[pt]] if kc == 0 else (), partial=() if kc == 0 else [PSB[pt]])
                    S.op("act", lambda e: e.activation(out=xeT[ix][:, :, half * 128:(half + 1) * 128], in_=ptv, func=AF.Copy),
                         reads=[PSB[pt]], partial=[BxeT[ix]])

                def S3(s_, fc):
                    ex_, b = steps[s_]
                    ie, ix = ex_ % 2, s_ % 2
                    pg, pu = (0, 1) if fc % 2 == 0 else (2, 3)
                    for kc in range(8):
                        mm(PS[pg][:, 0:256], wg[ie][:, kc, fc * 128:(fc + 1) * 128], xeT[ix][:, kc, :], kc == 0, kc == 7,
                           [Bwg[ie], BxeT[ix]], [PSB[pg]])
                    for kc in range(8):
                        mm(PS[pu][:, 0:256], wu[ie][:, kc, fc * 128:(fc + 1) * 128], xeT[ix][:, kc, :], kc == 0, kc == 7,
                           [Bwu[ie], BxeT[ix]], [PSB[pu]])
                    ks = fc % 2
                    S.op("act", lambda e: e.activation(out=sg[ks][:], in_=PS[pg][:, 0:256], func=AF.Silu),
                         reads=[PSB[pg]], writes=[Bsg[ks]])
                    S.op("dve", lambda e: e.tensor_tensor(out=actT[ix][:, fc, :], in0=PS[pu][:, 0:256], in1=sg[ks][:], op=ALU.mult),
                         reads=[PSB[pu], Bsg[ks]], partial=[BactT[ix]])

                def S4(s_, half):
                    if s_ < 0:
                        return
                    ex_, b = steps[s_]
                    ie, ix = ex_ % 2, s_ % 2
                    ky = cc_["y"] % 3
                    cc_["y"] += 1
                    for nh in range(2):
                        py = 4 + (cc_["p"] % 2)
                        cc_["p"] += 1
                        for fc in range(4):
                            mm(PS[py][:, :], actT[ix][:, fc, half * 128:(half + 1) * 128], wd[ie][:, fc, nh * 512:(nh + 1) * 512],
                               fc == 0, fc == 3, [BactT[ix], Bwd[ie]], [PSB[py]])
                        S.op("dve", lambda e, py=py, nh=nh: e.scalar_tensor_tensor(
                            out=yo[ky][:, nh * 512:(nh + 1) * 512], in0=PS[py][:, :], scalar=gate_all[:, b, ex_, half:half + 1],
                            in1=g2bc[b][:, nh * 512:(nh + 1) * 512], op0=ALU.mult, op1=ALU.mult),
                            reads=[PSB[py], Bidx, Bg2[b]], partial=[Byo[ky]])
                    S.dma("pool", lambda e: e.indirect_dma_start(
                        out=out.rearrange("b t d -> (b t) d"), out_offset=bass.IndirectOffsetOnAxis(ap=idx_all[:, b, ex_, half:half + 1], axis=0),
                        in_=yo[ky][:], in_offset=None, compute_op=ALU.add),
                        reads=[Byo[ky], Bidx], writes=[Bout[b]])

                w_load(0)
                for j in range(3):
                    w_cast(0, j)
                S1(0)
                S1(1)
                S2(0, 0)
                S2(0, 1)
                for s_ in range(NS + 1):
                    if s_ < NS:
                        ex_, b = steps[s_]
                        if b == 0:
                            w_load(ex_ + 1)
                        else:
                            w_cast(ex_ + 1, b - 1)
                    S1(s_ + 2)
                    if s_ < NS:
                        S3(s_, 0)
                    S2(s_ + 1, 0)
                    if s_ < NS:
                        S3(s_, 1)
                    S4(s_ - 1, 0)
                    if s_ < NS:
                        S3(s_, 2)
                    S2(s_ + 1, 1)
                    if s_ < NS:
                        S3(s_, 3)
                    S4(s_ - 1, 1)
            S.barrier()
        if run("G"):
            phase_G()

        with nc.allow_low_precision("fp32 math, bf16 storage of matmul operands"):
            S.finish()
        print("instructions:", S.ninst)
    return nc


def _consts():
    bf = ml_dtypes.bfloat16
    t = np.arange(T)
    inv = (1.0 / (np.float32(10000.0) ** (np.arange(8, dtype=np.float32) / np.float32(8)))).astype(np.float32)
    pr = (t // 64).astype(np.float32)
    pcl = (t % 64).astype(np.float32)

    def tab(pos):
        ang = pos[:, None] * inv[None, :]
        ang = np.concatenate([ang, ang], axis=-1).astype(np.float32)
        return np.cos(ang).astype(np.float32), np.sin(ang).astype(np.float32)

    cr, sr = tab(pr)
    cc_, scc = tab(pcl)
    cos_t = np.concatenate([cr, cc_], axis=-1)
    sgn = np.concatenate([-np.ones(8), np.ones(8)]).astype(np.float32)
    sin_t = np.concatenate([sr * sgn, scc * sgn], axis=-1)
    to_tiles = lambda a: np.ascontiguousarray(a.reshape(NT, 128, 32).transpose(1, 0, 2))
    prod = (np.outer(t, t) % T).astype(np.float64) * (2 * np.pi / T)
    dft_c = np.cos(prod).astype(bf)
    dft_s = np.sin(prod).astype(bf)
    f = np.arange(128)
    pch = (np.outer(f, f) % 128).astype(np.float64) * (2 * np.pi / 128)
    rvc = np.zeros((128, NT, NE, 2), np.float32)
    rvc[:, :, :, 0] = np.arange(NT)[None, :, None]
    rvc[:, :, :, 1] = np.arange(128)[:, None, None]
    tri = (np.arange(128)[:, None] < np.arange(128)[None, :]).astype(np.float32)
    return dict(
        rope_cos=to_tiles(cos_t), rope_sin=to_tiles(sin_t), dft_c=dft_c, dft_s=dft_s,
        ch_c=np.cos(pch).astype(np.float32), ch_s=np.sin(pch).astype(np.float32),
        ident_b=np.eye(128, dtype=np.float32).astype(bf), ident_f=np.eye(128, dtype=np.float32),
        iota256=np.ascontiguousarray(np.broadcast_to(np.arange(256, dtype=np.float32)[None, :], (128, 256))),
        tri=tri.astype(bf), rvc=rvc.astype(bf),
    )


def make_in_maps(x, c, ctx, c_ctx, w_mod, b_mod, norm1_g, w_in, q_a_norm_g, kv_a_norm_g, w_q_up, w_kv_up, q_norm_g,
                 k_norm_g, w_o_attn, w_fourier, w_out, norm2_g, w_router, w_e_gate, w_e_up, w_e_down):
    f32 = lambda a: np.ascontiguousarray(np.asarray(a, dtype=np.float32))
    consts = _consts()
    fm = lambda v, k: np.ascontiguousarray(f32(v).reshape(k, 128).T)
    bc = lambda v: np.ascontiguousarray(np.broadcast_to(f32(v)[None, :], (128, v.shape[-1])))
    shared = dict(
        bmod5=np.ascontiguousarray(np.broadcast_to(f32(b_mod[0])[None, :], (5, 6 * D))),
        n1g=fm(norm1_g[0], 8), n2g=fm(norm2_g[0], 8), n2g_bc=bc(norm2_g[0]),
        qag=fm(q_a_norm_g[0], 3), kvag=fm(kv_a_norm_g[0], 2), qg_bc=bc(q_norm_g[0]), kg_bc=bc(k_norm_g[0]),
        w_mod=f32(w_mod[0]), w_in=f32(w_in[0]), w_q_up=f32(w_q_up[0]), w_kv_up=f32(w_kv_up[0]),
        w_o_attn=f32(w_o_attn[0]), w_fourier=f32(w_fourier[0]), w_out=f32(w_out[0]), w_router=f32(w_router[0]),
        w_e_gate=f32(w_e_gate[0]), w_e_up=f32(w_e_up[0]), w_e_down=f32(w_e_down[0]), **consts)
    x = f32(x)
    ctx = f32(ctx)
    c = f32(c)
    c_ctx = f32(c_ctx)
    maps = []
    for core in range(8):
        sl = slice(core * NB, (core + 1) * NB)
        cc5 = np.concatenate([c[sl], c_ctx[None, :]], axis=0)
        cTl = np.ascontiguousarray(cc5.reshape(5, 8, 128).transpose(2, 1, 0).reshape(128, 40))
        maps.append(dict(x4=x[sl], ctx4=ctx[sl], cT=cTl, **shared))
    return maps


def kernel(**inputs):
    nc = build_program()
    maps = make_in_maps(**inputs)
    res = run_bass_kernel_spmd(nc, maps, core_ids=list(range(8)))
    return np.concatenate([np.asarray(r["out"], dtype=np.float32) for r in res.results], axis=0)
```
